# Optimizing a Trainium2 kernel written in Bass

```python
import math
import jax, jax.numpy as jnp
from jax import lax
import numpy as np

D_MODEL = 1024
BATCH = 16
SEQ = 2048
DEPTH = 2
DEC_BATCH = 16
DEC_SEQ = 64
PAST_LEN = 4096

CHUNK = 64
N_A = DEPTH // 2
N_B = DEPTH - N_A
EPS = 1e-6
D_INNER = 2 * D_MODEL
SSM_HEAD_DIM = 64
SSM_HEADS = D_INNER // SSM_HEAD_DIM
SSM_GROUPS = 4
HEADS_PER_GROUP = SSM_HEADS // SSM_GROUPS
SSM_STATE = 128
CONV_W = 4
CONV_DIM = D_INNER + 2 * SSM_GROUPS * SSM_STATE
IN_PROJ = D_INNER + CONV_DIM + SSM_HEADS
ATT_HEAD_DIM = 64
ATT_HEADS = D_MODEL // ATT_HEAD_DIM
ATT_DIM = ATT_HEADS * ATT_HEAD_DIM
BAND_CHUNKS = 8
BAND_PAST = BAND_CHUNKS * CHUNK
REL_CLIP = 256
FFN_HIDDEN = ((8 * D_MODEL + 3 * 256 - 1) // (3 * 256)) * 256

kernel_name = 'hybrid_ssd_chunkband_stream_step'


def rmsnorm(x, w):
    xf = x.astype(jnp.float32)
    y = xf * lax.rsqrt(jnp.mean(xf * xf, axis=-1, keepdims=True) + EPS)
    return (y * w.astype(jnp.float32)).astype(x.dtype)


def swiglu(u, w_in, w_out):
    g, up = jnp.split(u @ w_in, 2, axis=-1)
    return (jax.nn.silu(g) * up) @ w_out


def ssd_scan(x, dt, A, Bm, Cm, h0):
    b, L = x.shape[:2]
    Q = CHUNK if L % CHUNK == 0 else L
    nc = L // Q
    G, J, P, N = SSM_GROUPS, HEADS_PER_GROUP, SSM_HEAD_DIM, SSM_STATE
    xd = (x * dt[..., None]).reshape(b, nc, Q, G, J, P)
    Bc = Bm.reshape(b, nc, Q, G, N)
    Cc = Cm.reshape(b, nc, Q, G, N)
    dA = jnp.moveaxis((dt * A).reshape(b, nc, Q, G, J), 2, -1)
    Acs = jnp.cumsum(dA, axis=-1)
    causal = jnp.tril(jnp.ones((Q, Q), dtype=bool))
    Lm = jnp.exp(jnp.where(causal, Acs[..., :, None] - Acs[..., None, :], -jnp.inf))
    CB = jnp.einsum('bcqgn,bcsgn->bcgqs', Cc, Bc)
    y_diag = jnp.einsum('bcgjqs,bcsgjp->bcqgjp', CB[:, :, :, None] * Lm, xd)
    decay_to_end = jnp.moveaxis(jnp.exp(Acs[..., -1:] - Acs), -1, 2)
    chunk_states = jnp.einsum('bcsgn,bcsgjp->cbgjpn', Bc, xd * decay_to_end[..., None])
    chunk_decay = jnp.moveaxis(jnp.exp(Acs[..., -1]), 1, 0)

    def step(h, inp):
        s, d = inp
        return h * d[..., None, None] + s, h

    h_final, h_prev = lax.scan(step, h0.reshape(b, G, J, P, N), (chunk_states, chunk_decay))
    y_off = jnp.einsum('bcqgn,cbgjpn->bcqgjp', Cc, h_prev) * jnp.moveaxis(jnp.exp(Acs), -1, 2)[..., None]
    y = (y_diag + y_off).reshape(b, L, SSM_HEADS, P)
    return y, h_final.reshape(b, SSM_HEADS, P, N)


def ssd_mixer(u, h0, conv_prev, w_in, conv_w, conv_b, dt_bias, A_log, D_skip, gnorm_w, w_out):
    b, L, _ = u.shape
    zxbcdt = u @ w_in
    z = zxbcdt[..., :D_INNER]
    xbc = zxbcdt[..., D_INNER:D_INNER + CONV_DIM]
    dt_raw = zxbcdt[..., D_INNER + CONV_DIM:]
    padded = jnp.concatenate([conv_prev.astype(xbc.dtype), xbc], axis=1)
    conv = conv_b
    for k in range(CONV_W):
        conv = conv + padded[:, k:k + L] * conv_w[k]
    new_conv = padded[:, L:]
    xbc = jax.nn.silu(conv).astype(jnp.float32)
    xs = xbc[..., :D_INNER].reshape(b, L, SSM_HEADS, SSM_HEAD_DIM)
    Bm = xbc[..., D_INNER:D_INNER + SSM_GROUPS * SSM_STATE].reshape(b, L, SSM_GROUPS, SSM_STATE)
    Cm = xbc[..., D_INNER + SSM_GROUPS * SSM_STATE:].reshape(b, L, SSM_GROUPS, SSM_STATE)
    dt = jax.nn.softplus(dt_raw.astype(jnp.float32) + dt_bias.astype(jnp.float32))
    A = -jnp.exp(A_log.astype(jnp.float32))
    y, h_final = ssd_scan(xs, dt, A, Bm, Cm, h0.astype(jnp.float32))
    y = y + xs * D_skip.astype(jnp.float32)[:, None]
    y = y.reshape(b, L, D_INNER) * jax.nn.silu(z.astype(jnp.float32))
    yg = y.reshape(b, L, SSM_GROUPS, D_INNER // SSM_GROUPS)
    yg = yg * lax.rsqrt(jnp.mean(yg * yg, axis=-1, keepdims=True) + EPS)
    y = (yg.reshape(b, L, D_INNER) * gnorm_w.astype(jnp.float32)).astype(u.dtype)
    return y @ w_out, h_final.astype(u.dtype), new_conv


def shared_kv(x, kv_norm_w, w_kv):
    b, L, _ = x.shape
    kv = (rmsnorm(x, kv_norm_w) @ w_kv).reshape(b, L, 2, ATT_HEADS, ATT_HEAD_DIM)
    return kv[:, :, 0], kv[:, :, 1]


def band_attend(q, k, v, qpos, kpos, rel_bias):
    dist = jnp.clip(qpos[:, None] - kpos[None, :], -REL_CLIP, REL_CLIP) + REL_CLIP
    bias = rel_bias[:, dist].astype(jnp.float32)
    qc = qpos // CHUNK
    kc = kpos // CHUNK
    valid = (kpos[None, :] >= 0) & (kc[None, :] <= qc[:, None]) & (kc[None, :] >= qc[:, None] - BAND_CHUNKS)
    s = jnp.einsum('bqhd,bkhd->bhqk', q, k).astype(jnp.float32) * (ATT_HEAD_DIM ** -0.5) + bias
    s = jnp.where(valid, s, -jnp.inf)
    p = jax.nn.softmax(s, axis=-1).astype(v.dtype)
    return jnp.einsum('bhqk,bkhd->bqhd', p, v)


def prompt_band_attention(q, k, v, rel_bias):
    b, L = q.shape[:2]
    nc = L // CHUNK
    band = BAND_PAST + CHUNK
    pad = jnp.zeros((b, BAND_PAST, ATT_HEADS, ATT_HEAD_DIM), k.dtype)
    kp = jnp.concatenate([pad, k], axis=1)
    vp = jnp.concatenate([pad, v], axis=1)

    def one_chunk(c):
        start = c * CHUNK
        qc = lax.dynamic_slice_in_dim(q, start, CHUNK, axis=1)
        kc = lax.dynamic_slice_in_dim(kp, start, band, axis=1)
        vc = lax.dynamic_slice_in_dim(vp, start, band, axis=1)
        qpos = start + jnp.arange(CHUNK)
        kpos = start - BAND_PAST + jnp.arange(band)
        return band_attend(qc, kc, vc, qpos, kpos, rel_bias)

    out = lax.map(one_chunk, jnp.arange(nc))
    return jnp.moveaxis(out, 0, 1).reshape(b, L, ATT_DIM)


def sample_band_attention(q, k_new, v_new, k_cache, v_cache, rel_bias):
    b, L = q.shape[:2]
    R = k_cache.shape[1]
    k = jnp.concatenate([k_cache.astype(k_new.dtype), k_new], axis=1)
    v = jnp.concatenate([v_cache.astype(v_new.dtype), v_new], axis=1)
    qpos = PAST_LEN + jnp.arange(L)
    kpos = jnp.concatenate([PAST_LEN - R + jnp.arange(R), PAST_LEN + jnp.arange(L)])
    return band_attend(q, k, v, qpos, kpos, rel_bias).reshape(b, L, ATT_DIM)


def setup_inputs(seed: int = 0) -> dict:
    key = jax.random.key(seed)
    ks = jax.random.split(key, 24)
    f32 = jnp.float32

    def nrm(k, shape, scale):
        return scale * jax.random.normal(k, shape, f32)

    cache_rows = min(BAND_PAST, PAST_LEN)
    dt0 = jnp.exp(jax.random.uniform(ks[10], (N_A, SSM_HEADS), f32, math.log(1e-3), math.log(1e-1)))
    dt_bias = dt0 + jnp.log(-jnp.expm1(-dt0))
    A_log = jnp.log(jax.random.uniform(ks[11], (N_A, SSM_HEADS), f32, 1.0, 16.0))
    return {
        'x_prompt': nrm(ks[0], (BATCH, SEQ, D_MODEL), 1.0),
        'x_sample': nrm(ks[1], (DEC_BATCH, DEC_SEQ, D_MODEL), 1.0),
        'state_ssm': nrm(ks[2], (N_A, DEC_BATCH, SSM_HEADS, SSM_HEAD_DIM, SSM_STATE), 0.5),
        'state_conv': nrm(ks[3], (N_A, DEC_BATCH, CONV_W - 1, CONV_DIM), 1.0),
        'cache_k': nrm(ks[4], (DEC_BATCH, cache_rows, ATT_HEADS, ATT_HEAD_DIM), 1.0),
        'cache_v': nrm(ks[5], (DEC_BATCH, cache_rows, ATT_HEADS, ATT_HEAD_DIM), 1.0),
        'norm_w': 1.0 + nrm(ks[6], (DEPTH, 4, D_MODEL), 0.05),
        'ssm_w_in': nrm(ks[7], (N_A, D_MODEL, IN_PROJ), D_MODEL ** -0.5),
        'ssm_conv_w': nrm(ks[8], (N_A, CONV_W, CONV_DIM), CONV_W ** -0.5),
        'ssm_conv_b': nrm(ks[9], (N_A, CONV_DIM), 0.02),
        'ssm_dt_bias': dt_bias,
        'ssm_A_log': A_log,
        'ssm_D': 1.0 + nrm(ks[12], (N_A, SSM_HEADS), 0.1),
        'ssm_norm_w': 1.0 + nrm(ks[13], (N_A, D_INNER), 0.05),
        'ssm_w_out': nrm(ks[14], (N_A, D_INNER, D_MODEL), D_INNER ** -0.5),
        'kv_norm_w': 1.0 + nrm(ks[15], (D_MODEL,), 0.05),
        'w_kv': nrm(ks[16], (D_MODEL, 2 * ATT_DIM), D_MODEL ** -0.5),
        'attn_w_q': nrm(ks[17], (N_B, D_MODEL, ATT_DIM), D_MODEL ** -0.5),
        'attn_rel_bias': nrm(ks[18], (N_B, ATT_HEADS, 2 * REL_CLIP + 1), 0.3),
        'attn_w_o': nrm(ks[19], (N_B, ATT_DIM, D_MODEL), ATT_DIM ** -0.5),
        'ffn_w_in': nrm(ks[20], (DEPTH, D_MODEL, 2 * FFN_HIDDEN), D_MODEL ** -0.5),
        'ffn_w_out': nrm(ks[21], (DEPTH, FFN_HIDDEN, D_MODEL), FFN_HIDDEN ** -0.5),
    }


def reference(x_prompt, x_sample, state_ssm, state_conv, cache_k, cache_v, norm_w,
              ssm_w_in, ssm_conv_w, ssm_conv_b, ssm_dt_bias, ssm_A_log, ssm_D, ssm_norm_w, ssm_w_out,
              kv_norm_w, w_kv, attn_w_q, attn_rel_bias, attn_w_o, ffn_w_in, ffn_w_out):
    xp, xs = x_prompt, x_sample
    bp, bs = xp.shape[0], xs.shape[0]
    ssm_p, conv_p, ssm_s, conv_s = [], [], [], []
    kp = vp = ks = vs = None
    for layer in range(DEPTH):
        nw = norm_w[layer]
        if layer < N_A:
            a = layer
            params = (ssm_w_in[a], ssm_conv_w[a], ssm_conv_b[a], ssm_dt_bias[a], ssm_A_log[a],
                      ssm_D[a], ssm_norm_w[a], ssm_w_out[a])
            h0 = jnp.zeros((bp, SSM_HEADS, SSM_HEAD_DIM, SSM_STATE), jnp.float32)
            c0 = jnp.zeros((bp, CONV_W - 1, CONV_DIM), xp.dtype)
            mp, hp_new, cp_new = ssd_mixer(rmsnorm(xp, nw[0]), h0, c0, *params)
            ms, hs_new, cs_new = ssd_mixer(rmsnorm(xs, nw[0]), state_ssm[a], state_conv[a], *params)
            ssm_p.append(hp_new)
            conv_p.append(cp_new)
            ssm_s.append(hs_new)
            conv_s.append(cs_new)
        else:
            if layer == N_A:
                kp, vp = shared_kv(xp, kv_norm_w, w_kv)
                ks, vs = shared_kv(xs, kv_norm_w, w_kv)
            i = layer - N_A
            qp = (rmsnorm(xp, nw[0]) @ attn_w_q[i]).reshape(bp, -1, ATT_HEADS, ATT_HEAD_DIM)
            qs = (rmsnorm(xs, nw[0]) @ attn_w_q[i]).reshape(bs, -1, ATT_HEADS, ATT_HEAD_DIM)
            mp = prompt_band_attention(qp, kp, vp, attn_rel_bias[i]) @ attn_w_o[i]
            ms = sample_band_attention(qs, ks, vs, cache_k, cache_v, attn_rel_bias[i]) @ attn_w_o[i]
        xp = xp + rmsnorm(mp, nw[1])
        xs = xs + rmsnorm(ms, nw[1])
        xp = xp + rmsnorm(swiglu(rmsnorm(xp, nw[2]), ffn_w_in[layer], ffn_w_out[layer]), nw[3])
        xs = xs + rmsnorm(swiglu(rmsnorm(xs, nw[2]), ffn_w_in[layer], ffn_w_out[layer]), nw[3])
    rows_p = min(BAND_PAST, xp.shape[1])
    return (xp, xs, jnp.stack(ssm_p), jnp.stack(conv_p), kp[:, -rows_p:], vp[:, -rows_p:],
            jnp.stack(ssm_s), jnp.stack(conv_s), ks, vs)
```

```python
import numpy as np
from contextlib import ExitStack
import concourse.bass as bass
import concourse.mybir as mybir
from concourse.ap import AP
from concourse.bass_utils import run_bass_kernel_spmd

F32 = mybir.dt.float32
BF16 = mybir.dt.bfloat16
ALU = mybir.AluOpType
AF = mybir.ActivationFunctionType

D = 1024
DI = 2048
NH = 32
HP = 64
NG = 4
NS = 128
CD = 3072
INP = 5152
FH = 2816
AH = 16
EPS = 1e-6
SB_BASE = 16512
SB_END = 229344
GRAN = 64
NEG = -30000.0


class Op:
    __slots__ = ("eng", "fn", "idx", "waits", "dwaits", "needs_inc", "semval", "dma", "dsem", "dval", "phase")

    def __init__(self, eng, fn, dma):
        self.eng = eng
        self.fn = fn
        self.dma = dma
        self.waits = []
        self.dwaits = []
        self.needs_inc = False
        self.semval = None
        self.dsem = None
        self.dval = None


class Sched:
    ENGS = ("pe", "act", "dve", "pool", "sp")
    NDSEM = 24

    def __init__(self):
        self.ops = {e: [] for e in self.ENGS}
        self.lastw = {}
        self.readers = {}
        self.waited = {e: {e2: -1 for e2 in self.ENGS} for e in self.ENGS}
        self.dwaited = {e: {} for e in self.ENGS}
        self.dcount = {e: 0 for e in self.ENGS}
        self.sb_addr = {}
        self.ps_names = set()
        self.phase = 'setup'

    def keys(self, item):
        if isinstance(item, (str, tuple)):
            return [item]
        name = item.name
        if name in self.sb_addr:
            base, ds = self.sb_addr[name]
            apl = item.ap
            pstep = apl[0][0]
            off = int(item.offset) % pstep if pstep > 0 else int(item.offset)
            lo = off
            hi = off
            for st, cnt in apl[1:]:
                if st >= 0:
                    hi += st * (cnt - 1)
                else:
                    lo += st * (cnt - 1)
            b0 = (base + lo * ds) // GRAN
            b1 = (base + (hi + 1) * ds - 1) // GRAN
            return [("s", g) for g in range(b0, b1 + 1)]
        if name in self.ps_names:
            return [("p", name)]
        return [("d", name, int(item.offset))]

    def add(self, eng, fn, reads=(), writes=(), dma=False):
        op = Op(eng, fn, dma)
        op.idx = len(self.ops[eng])
        op.phase = self.phase
        rk = []
        for r in reads:
            rk.extend(self.keys(r))
        wk = []
        for w in writes:
            wk.extend(self.keys(w))
        deps = []
        for k in rk:
            w = self.lastw.get(k)
            if w is not None:
                deps.append(w)
        for k in wk:
            w = self.lastw.get(k)
            if w is not None:
                deps.append(w)
            deps.extend(self.readers.get(k, ()))
        need = {}
        for d in deps:
            if d.dma:
                key = (d.eng, d.dsem)
                if self.dwaited[eng].get(key, 0) < d.dval:
                    self.dwaited[eng][key] = d.dval
                    op.dwaits.append((d.eng, d.dsem, d.dval))
            else:
                if d.eng == eng and eng == "pe":
                    continue
                if need.get(d.eng, -1) < d.idx:
                    need[d.eng] = d.idx
        for e2, idx in need.items():
            if self.waited[eng][e2] < idx:
                self.waited[eng][e2] = idx
                p = self.ops[e2][idx]
                p.needs_inc = True
                op.waits.append(p)
        if dma:
            n = self.dcount[eng]
            self.dcount[eng] = n + 1
            op.dsem = n % self.NDSEM
            op.dval = 16 * (n // self.NDSEM + 1)
            if n >= self.NDSEM:
                key = (eng, op.dsem)
                if self.dwaited[eng].get(key, 0) < op.dval - 16:
                    self.dwaited[eng][key] = op.dval - 16
                    op.dwaits.append((eng, op.dsem, op.dval - 16))
        for k in rk:
            self.readers.setdefault(k, []).append(op)
        for k in wk:
            self.lastw[k] = op
            self.readers[k] = []
        self.ops[eng].append(op)
        return op

    def emit(self, nc):
        with ExitStack() as st:
            esem = {e: st.enter_context(nc.semaphore("s_" + e)) for e in self.ENGS}
            dsem = {}
            for e in self.ENGS:
                if self.dcount[e] > 0:
                    dsem[e] = [st.enter_context(nc.semaphore("d_%s_%d" % (e, i)))
                               for i in range(min(self.NDSEM, self.dcount[e]))]
            for e in self.ENGS:
                c = 0
                for op in self.ops[e]:
                    if op.needs_inc and not op.dma:
                        c += 1
                        op.semval = c
            block = st.enter_context(nc.Block())

            def run(e):
                def body(h):
                    for op in self.ops[e]:
                        for p in op.waits:
                            h.wait_ge(esem[p.eng], p.semval)
                        for (qe, s, v) in op.dwaits:
                            h.wait_ge(dsem[qe][s], v)
                        ins = op.fn(h)
                        if op.dma:
                            ins.then_inc(dsem[e][op.dsem], 16)
                        elif op.needs_inc:
                            ins.then_inc(esem[e], 1)
                    if e in dsem:
                        n = self.dcount[e]
                        for s in range(len(dsem[e])):
                            cnt = (n - s + self.NDSEM - 1) // self.NDSEM
                            if cnt > 0:
                                h.wait_ge(dsem[e][s], 16 * cnt)
                return body
            block.tensor(run("pe"))
            block.scalar(run("act"))
            block.vector(run("dve"))
            block.gpsimd(run("pool"))
            block.sync(run("sp"))


_CACHE = {}


def build(n_ptiles=4, with_sample=True, dbg=None, upto='all', upto_s=None):
    LP = n_ptiles * 512
    nc = bass.Bass("TRN2", target_bir_lowering=False)
    S = Sched()

    def din(name, shape, dt=F32):
        return nc.dram_tensor(name, shape, dt, kind="ExternalInput").ap()

    def dout(name, shape):
        return nc.dram_tensor(name, shape, F32, kind="ExternalOutput").ap()

    def dscr(name, shape, dt):
        return nc.dram_tensor(name, shape, dt, kind="Internal").ap()

    xp_d = din("xp", [2, LP, D])
    xs_d = din("xs", [2, 64, D])
    sssm_d = din("sssm", [2, DI, NS])
    sconv_d = din("sconv", [2, 3, CD])
    ck_d = din("ck", [2, 512, D])
    cv_d = din("cv", [2, 512, D])
    normw_d = din("normw", [64, 128])
    win_d = din("win", [D, INP])
    convw_d = din("convw", [96, 128])
    convb_d = din("convb", [24, 128])
    dtb_d = din("dtb", [1, NH])
    alog_d = din("alog", [1, NH])
    dsk_d = din("dsk", [1, NH])
    gnw_d = din("gnw", [16, 128])
    wout_d = din("wout", [DI, D])
    kvnw_d = din("kvnw", [8, 128])
    wkv_d = din("wkv", [D, 2 * D])
    wq_d = din("wq", [D, D])
    relb_d = din("relb", [AH, 513])
    wo_d = din("wo", [D, D])
    fin_d = din("fin", [2, D, 2 * FH])
    fout_d = din("fout", [2, FH, D])

    yp_d = dout("yp", [2, LP, D])
    ys_d = dout("ys", [2, 64, D])
    ssmp_d = dout("ssmp", [2, DI, NS])
    convp_d = dout("convp", [2, 3, CD])
    kp_d = dout("kp", [2, 512, D])
    vp_d = dout("vp", [2, 512, D])
    ssms_d = dout("ssms", [2, DI, NS])
    convs_d = dout("convs", [2, 3, CD])
    ks_d = dout("ks", [2, 64, D])
    vs_d = dout("vs", [2, 64, D])
    dbg_d = {}
    if dbg:
        for k, (shp, dt_) in dbg.items():
            dbg_d[k] = nc.dram_tensor("dbg_" + k, shp, dt_, kind="ExternalOutput").ap()

    NSLAB = 90
    wscr = dscr("wscr", [NSLAB, 128, 4096], BF16)
    Fsc = dscr("Fsc", [AH, 128, 896], F32)

    pos = [SB_BASE]
    cnt = [0]

    def nbytes_of(shape, dt):
        ds = 2 if dt == BF16 else 4
        n = 1
        for s in shape:
            n *= s
        return (n * ds + 63) // 64 * 64

    def alloc(shape, dt, at=None):
        nb_ = nbytes_of(shape, dt)
        if at is None:
            addr = pos[0]
            pos[0] += nb_
        else:
            addr = at
        assert addr + nb_ <= SB_END, (addr, nb_, shape)
        cnt[0] += 1
        name = "t%d" % cnt[0]
        h = nc.alloc_sbuf_tensor_at(name, [128] + list(shape), dt, offset=addr)
        S.sb_addr[h[:].name] = (addr, 2 if dt == BF16 else 4)
        return h

    est = ExitStack()
    ps = [est.enter_context(nc.psum_tensor("psb%d" % i, [128, 512], F32)) for i in range(8)]
    for i in range(8):
        S.ps_names.add(ps[i][:].name)
    prot = [0]

    rot = [6]

    def nps():
        b_ = ps[prot[0] % rot[0]]
        prot[0] += 1
        return b_

    xT = alloc([8, 512], F32)
    hT = alloc([NH, HP], F32)
    hTb2 = [alloc([NH, HP], BF16) for _ in range(2)]
    convst = alloc([24, 4], F32)
    KT = [alloc([8, 512], BF16) for _ in range(2)]
    VW = 1536
    Vt = [alloc([4, VW], BF16) for _ in range(2)]
    TBb = alloc([AH, 640], BF16)
    identF = alloc([128], F32)
    identB = alloc([128], BF16)
    onesB = alloc([128], BF16)
    Umask = alloc([128], F32)
    Lb = alloc([128], BF16)
    pcol = alloc([256], F32)
    dtb_bc = alloc([NH], F32)
    A_bc = alloc([NH], F32)
    Dcol = alloc([16], F32)
    ub = alloc([8, 512], BF16)
    slabs = [alloc([4096], BF16) for _ in range(4)]
    rA = alloc([512], F32)
    lnt = alloc([512], F32)
    dt_all = alloc([4, NH], F32)
    arena0 = pos[0]

    class Ar:
        def __init__(self, start):
            self.p = start

        def al(self, shape, dt):
            t_ = alloc(shape, dt, at=self.p)
            self.p += nbytes_of(shape, dt)
            return t_

    A = Ar(arena0)
    sq = [A.al([512], BF16) for _ in range(2)]
    aftersq = A.p
    ynorm = A.al([16, 512], BF16)
    gstart = A.p
    zs = A.al([4, 512], F32)
    xbc = A.al([2, 528], F32)
    xsT = A.al([4, 512], F32)
    BTf = A.al([512], F32)
    CTf = A.al([512], F32)
    BTb = A.al([512], BF16)
    CTb = A.al([512], BF16)
    dA = A.al([8], F32)
    dtd = A.al([8], F32)
    rhsD = A.al([8, 128], BF16)
    Mb2 = [A.al([8, 128], BF16) for _ in range(2)]
    Cs2 = [A.al([8, 128], BF16) for _ in range(2)]
    CBm = A.al([128], F32)
    xd2 = [A.al([512], BF16) for _ in range(2)]
    xdp2 = [A.al([512], BF16) for _ in range(2)]
    Btok2 = [A.al([128], BF16) for _ in range(2)]
    dec2 = [A.al([8], F32) for _ in range(2)]
    Ebuf = A.al([4, 128], F32)
    Evbuf = A.al([4, 128], F32)
    stmp = A.al([512], F32)
    crow = [A.al([128], F32) for _ in range(2)]
    endA = A.p
    print('arena end A', endA, 'limit', SB_END)
    A2 = Ar(gstart)
    mT = A2.al([8, 512], F32)
    tmpn3 = [A2.al([512], F32) for _ in range(3)]
    afterm = A2.p
    hid = A2.al([22, 512], BF16)
    sg = [A2.al([512], F32) for _ in range(2)]
    endB = A2.p
    A3 = Ar(afterm)
    u2 = A3.al([8, 512], BF16)
    QT = A3.al([8, 512], BF16)
    Pt = [A3.al([512], BF16) for _ in range(3)]
    rden = A3.al([512], F32)
    KVst = alloc([4, D], F32, at=aftersq)
    endC = A3.p
    print('arena end B', endB, 'C', endC, 'A4', gstart + 16384 + 8192)
    OT = ynorm
    xin = alloc([4, D], F32, at=aftersq)
    A4 = Ar(gstart)
    xtok = A4.al([4, D], F32)
    stst = A4.al([16, 128], F32)
    A5 = Ar(aftersq)
    prow = A5.al([2, 128], F32)
    cbt = A5.al([AH, 1], F32)
    tailt = A5.al([AH, 383], F32)
    TB32 = A5.al([AH, 640], F32)

    PC_NORM, PC_KVN, PC_GN, PC_CB, PC_CW = 0, 64, 72, 88, 112

    def ncol(l, j, c):
        i = PC_NORM + (l * 4 + j) * 8 + c
        return pcol[:, i:i + 1]

    def cwcol(k, ch):
        i = PC_CW + k * 24 + ch
        return pcol[:, i:i + 1]

    def dma(eng, out, in_, reads=None, writes=None, **kw):
        return S.add(eng, lambda e: e.dma_start(out=out, in_=in_, **kw),
                     reads=[in_] if reads is None else reads,
                     writes=[out] if writes is None else writes, dma=True)

    def mm(out, lhsT, rhs, start=True, stop=True):
        return S.add("pe", lambda e: e.matmul(out, lhsT=lhsT, rhs=rhs, start=start, stop=stop),
                     reads=[lhsT, rhs], writes=[out])

    def tr(out, in_, n):
        return S.add("pe", lambda e: e.transpose(out, in_, identF[0:n, 0:n]), reads=[in_, identF[:]], writes=[out])

    def act(out, in_, func, bias=0.0, scale=1.0):
        rd = [in_]
        if not isinstance(bias, float):
            rd.append(bias)
        if not isinstance(scale, float):
            rd.append(scale)
        return S.add("act", lambda e: e.activation(out=out, in_=in_, func=func, bias=bias, scale=scale),
                     reads=rd, writes=[out])

    def tt(eng, out, in0, in1, op):
        return S.add(eng, lambda e: e.tensor_tensor(out=out, in0=in0, in1=in1, op=op), reads=[in0, in1], writes=[out])

    def stt(eng, out, in0, scalar, in1, op0, op1):
        eng = "dve"
        return S.add(eng, lambda e: e.scalar_tensor_tensor(out=out, in0=in0, scalar=scalar, in1=in1, op0=op0, op1=op1),
                     reads=[in0, scalar, in1], writes=[out])

    def ts(eng, out, in0, s1, s2, op0, op1=None):
        rd = [in0]
        if not isinstance(s1, float):
            rd.append(s1)
        if s2 is not None and not isinstance(s2, float):
            rd.append(s2)
        if op1 is None:
            return S.add(eng, lambda e: e.tensor_scalar(out=out, in0=in0, scalar1=s1, scalar2=None, op0=op0), reads=rd, writes=[out])
        return S.add(eng, lambda e: e.tensor_scalar(out=out, in0=in0, scalar1=s1, scalar2=s2, op0=op0, op1=op1), reads=rd, writes=[out])

    def cp(eng, out, in_):
        if eng == "act":
            return S.add("act", lambda e: e.copy(out=out, in_=in_), reads=[in_], writes=[out])
        return S.add(eng, lambda e: e.tensor_copy(out=out, in_=in_), reads=[in_], writes=[out])

    def mset(eng, out, val):
        return S.add(eng, lambda e: e.memset(out, val), writes=[out])

    rr = [0]

    def ew2():
        rr[0] += 1
        return ("dve", "pool")[rr[0] % 2]

    def dump(key, src, idx=None):
        if dbg and key in dbg_d:
            o = dbg_d[key] if idx is None else dbg_d[key][idx]
            dma("sp", o, src)

    cast_jobs = {}
    cast_done = set()

    def need_cast(keys):
        for k in keys:
            if k not in cast_done:
                cast_done.add(k)
                dst, src = cast_jobs[k]
                dma("pool", dst, src, writes=[k])

    def reg(key, dst, src):
        cast_jobs[key] = (dst, src)
        return [key]

    mset("pool", identF[:], 1.0)
    S.add("pool", lambda e: e.affine_select(out=identF[:], in_=identF[:], pattern=[[-1, 128]], compare_op=ALU.is_equal,
                                            fill=0.0, base=0, channel_multiplier=1), reads=[identF[:]], writes=[identF[:]])
    cp("dve", identB[:], identF[:])
    for vb in range(2):
        for nb_ in range(4):
            mset("pool", Vt[vb][:, nb_, :].rearrange("p (a c) -> p a c", c=192)[:, :, 64:128], 1.0)
    mset("dve", onesB[:], 1.0)
    mset("pool", Umask[:], 1.0)
    S.add("pool", lambda e: e.affine_select(out=Umask[:], in_=Umask[:], pattern=[[1, 128]], compare_op=ALU.is_ge,
                                            fill=0.0, base=0, channel_multiplier=-1), reads=[Umask[:]], writes=[Umask[:]])
    mset("pool", lnt[:, 0:128], 1.0)
    S.add("pool", lambda e: e.affine_select(out=lnt[:, 0:128], in_=lnt[:, 0:128], pattern=[[-1, 128]], compare_op=ALU.is_gt,
                                            fill=0.0, base=0, channel_multiplier=1), reads=[lnt[:, 0:128]], writes=[lnt[:, 0:128]])
    cp("dve", Lb[:], lnt[:, 0:128])

    mset("dve", prow[:], 0.0)
    dma("sp", prow[0:64, 0, :], normw_d)
    dma("sp", prow[64:72, 0, :], kvnw_d)
    dma("sp", prow[72:88, 0, :], gnw_d)
    dma("sp", prow[88:112, 0, :], convb_d)
    dma("sp", prow[112:128, 0, :], convw_d[0:16, :])
    dma("sp", prow[0:80, 1, :], convw_d[16:96, :])
    pb0 = nps()
    tr(pb0[:, 0:128], prow[:, 0, :], 128)
    tr(pb0[:, 128:256], prow[:, 1, :], 128)
    cp("dve", pcol[:], pb0[:, 0:256])

    dma("sp", dtb_bc[:], AP(dtb_d.tensor, 0, [[0, 128], [1, NH]]))
    dma("sp", A_bc[:], AP(alog_d.tensor, 0, [[0, 128], [1, NH]]))
    act(A_bc[:], A_bc[:], AF.Exp)
    ts("dve", A_bc[:], A_bc[:], -1.0, None, ALU.mult)
    dma("sp", Dcol[0:64, :], AP(dsk_d.tensor, 0, [[0, 64], [2, 16]]), allow_slow_non_contiguous=True)
    dma("sp", Dcol[64:128, :], AP(dsk_d.tensor, 1, [[0, 64], [2, 16]]), allow_slow_non_contiguous=True)

    dma("sp", Fsc[:, :, 0:513], relb_d.unsqueeze(1).broadcast_to([AH, 128, 513]), writes=["Fsc_a"])
    dma("sp", cbt[:], AP(relb_d.tensor, 512, [[0, 128], [513, AH], [1, 1]]), allow_slow_non_contiguous=True)
    cp("dve", tailt[:], cbt[:].broadcast_to([128, AH, 383]))
    dma("sp", Fsc[:, :, 513:896].rearrange("h p j -> p h j"), tailt[:], writes=["Fsc_b"])
    dma("sp", TB32[:], AP(Fsc.tensor, 256, [[895, 128], [128 * 896, AH], [1, 640]]), reads=["Fsc_a", "Fsc_b"])
    mset("dve", TB32[0:64, :, 576:640], NEG)
    mset("dve", TB32[64:128, :, 0:64], NEG)
    cp("dve", TBb[:], TB32[:])

    wq_list = []
    wstate = {"issued": 0}

    def slab_view(i, KC, w):
        return slabs[i % 4][:, 0:KC * w].rearrange("p (k w) -> p k w", k=KC)

    def cast_slab(si):
        if si in cast_done:
            return
        cast_done.add(si)
        KC, w, pieces = per_tile[si]
        dst = wscr[si][:, 0:KC * w].rearrange("p (k w) -> p k w", k=KC)
        for pi_, (src, c0, wp) in enumerate(pieces):
            dma("pool", dst[:, :, c0:c0 + wp], src.rearrange("(k p) w -> p k w", p=128), writes=[("wscr", si, pi_)])

    def issue_slab(i):
        npt = len(per_tile)
        for i2 in range(i, min(len(wq_list), i + 8)):
            cast_slab(i2 % npt)
        si = i % npt
        KC, w, pieces = per_tile[si]
        dma("sp", slabs[i % 4][:, 0:KC * w], wscr[si][:, 0:KC * w], reads=[("wscr", si, pi_) for pi_ in range(len(pieces))])

    def get_slab(i):
        while wstate["issued"] < min(len(wq_list), i + 3):
            issue_slab(wstate["issued"])
            wstate["issued"] += 1
        KC, w, _ = wq_list[i]
        return slab_view(i, KC, w)

    per_tile = []

    def tile_weight_list():
        L = []
        L.append((8, 32, [(win_d[:, INP - NH:INP], 0, NH)]))
        for g in range(NG):
            L.append((8, 512, [(win_d[:, g * 512:(g + 1) * 512], 0, 512)]))
            L.append((8, 512, [(win_d[:, DI + g * 512:DI + (g + 1) * 512], 0, 512)]))
            L.append((8, 256, [(win_d[:, 2 * DI + g * 128:2 * DI + (g + 1) * 128], 0, 128),
                               (win_d[:, 2 * DI + 512 + g * 128:2 * DI + 512 + (g + 1) * 128], 128, 128)]))
        for m in range(4):
            L.append((16, 256, [(wout_d[:, m * 256:(m + 1) * 256], 0, 256)]))
        for l in range(2):
            if l == 1:
                for m in range(4):
                    L.append((8, 512, [(wkv_d[:, m * 512:(m + 1) * 512], 0, 512)]))
                for m in range(2):
                    L.append((8, 512, [(wq_d[:, m * 512:(m + 1) * 512], 0, 512)]))
                for m in range(2):
                    L.append((8, 512, [(wo_d[:, m * 512:(m + 1) * 512], 0, 512)]))
            for j in range(6):
                w = 512 if j < 5 else 256
                L.append((8, w, [(fin_d[l][:, j * 512:j * 512 + w], 0, w)]))
                L.append((8, w, [(fin_d[l][:, FH + j * 512:FH + j * 512 + w], 0, w)]))
            for m in range(4):
                L.append((11, 256, [(fout_d[l][0:1408, m * 256:(m + 1) * 256], 0, 256)]))
                L.append((11, 256, [(fout_d[l][1408:2816, m * 256:(m + 1) * 256], 0, 256)]))
        assert len(L) <= NSLAB, len(L)
        return L

    per_tile.extend(tile_weight_list())

    nstat = [0, 0]

    class NormAcc:
        def __init__(self, n, T):
            self.bank = ps[6 + nstat[0] % 2]
            nstat[0] += 1
            self.i = 0
            self.n = n
            self.T = T
            self.pend = None

        def flush(self):
            if self.pend is not None:
                s_, k = self.pend
                mm(self.bank[:, 0:self.T], onesB[:], s_[:, 0:self.T], start=(k == 0), stop=(k == self.n - 1))
                self.pend = None

        def add(self, c, eng=None):
            T = self.T
            self.flush()
            s_ = sq[nstat[1] % 2]
            e = ("act", "pool", "dve")[nstat[1] % 3] if eng is None else eng
            nstat[1] += 1
            if e == "act":
                act(s_[:, 0:T], c, AF.Square)
            else:
                tt(e, s_[:, 0:T], c, c, ALU.mult)
            self.pend = (s_, self.i)
            self.i += 1

        def finish(self, Dn, rt):
            T = self.T
            assert self.i == self.n
            self.flush()
            act(lnt[:, 0:T], self.bank[:, 0:T], AF.Ln, bias=EPS, scale=1.0 / Dn)
            act(rt[:, 0:T], lnt[:, 0:T], AF.Exp, scale=-0.5)

    def norm_r(chunks, T, Dn, rt):
        a_ = NormAcc(len(chunks), T)
        for c in chunks:
            a_.add(c)
        a_.finish(Dn, rt)

    upto_g = upto
    xloaded = [False]

    def load_x(kind, b, ti):
        if kind == "p":
            dma("sp", xin[:], xp_d[b, ti * 512:(ti + 1) * 512, :].rearrange("(n p) d -> p n d", p=128))
        else:
            dma("sp", xin[0:64, 0, :], xs_d[b])
        xloaded[0] = True

    def run_tile(kind, b, ti, wbase, nxt=None):
        T = 512 if kind == "p" else 64
        TBk = min(T, 128)
        NB = (T + 127) // 128
        Q = TBk
        nblk = T // Q
        last_p = (kind == "p" and ti == n_ptiles - 1)
        need_out = last_p or kind == "s"
        wi = [wbase]
        upto = upto_g if (kind == "p" or upto_s is None) else upto_s

        def nslab():
            v = get_slab(wi[0])
            wi[0] += 1
            return v

        S.phase = 'load'
        if not xloaded[0]:
            load_x(kind, b, ti)
        xloaded[0] = False
        for c in range(8):
            pb_ = nps()
            for nb in range(NB):
                tr(pb_[:, nb * 128:nb * 128 + TBk], xin[0:TBk, nb, c * 128:(c + 1) * 128], TBk)
            cp(("act", "dve")[c % 2], xT[:, c, 0:T], pb_[:, 0:T])
        if kind == "s":
            cstage = xtok[0:3, :, :].rearrange("p a d -> p (a d)")
            dma("sp", cstage[:, 0:CD], sconv_d[b])
            pc_ = nps()
            for ch in range(24):
                tr(pc_[:, ch * 3:(ch + 1) * 3], cstage[:, ch * 128:(ch + 1) * 128], 3)
            cp("dve", convst[:, :, 0:3], pc_[:, 0:72].rearrange("p (c r) -> p c r", r=3))
            dma("sp", stst[:], sssm_d[b].rearrange("(c p) n -> p c n", p=128))
            for c4 in range(4):
                pb_ = nps()
                for i in range(4):
                    c = c4 * 4 + i
                    tr(pb_[:, i * 128:(i + 1) * 128], stst[:, c, :], 128)
                hv_ = hT[:, c4 * 8:(c4 + 1) * 8, :]
                cp("dve", hv_, pb_[:, :].rearrange("p (h d) -> p h d", h=8))
                cp("act", hTb2[1][:, c4 * 8:(c4 + 1) * 8, :], hv_)

        S.phase = 'norm0'
        norm_r([xT[:, c, 0:T] for c in range(8)], T, D, rA)
        for c in range(8):
            stt(ew2(), ub[:, c, 0:T], xT[:, c, 0:T], ncol(0, 0, c), rA[:, 0:T], ALU.mult, ALU.mult)
        if kind == "p" and b == 0 and ti == 0:
            dump("xT0", xT[:])
            dump("rA0", rA[:])
            dump("ub0", ub[:])
            dump("pcol", pcol[:])
        S.phase = 'dt'
        wdt = nslab()
        for nb in range(NB):
            pb_ = nps()
            for kc in range(8):
                mm(pb_[0:TBk, 0:NH], ub[:, kc, nb * 128:nb * 128 + TBk], wdt[:, kc, :], start=(kc == 0), stop=(kc == 7))
            tt("dve", dt_all[0:TBk, nb, :], pb_[0:TBk, 0:NH], dtb_bc[0:TBk, :], ALU.add)
        act(dt_all[0:TBk, 0:NB, :], dt_all[0:TBk, 0:NB, :], AF.Exp)
        act(dt_all[0:TBk, 0:NB, :], dt_all[0:TBk, 0:NB, :], AF.Ln, bias=1.0)

        for g in range(NG):
            chs = [g * 4 + m for m in range(4)] + [16 + g, 20 + g]
            S.phase = 'inproj_z'
            wz = nslab()
            for m in range(4):
                pb_ = nps()
                for kc in range(8):
                    mm(pb_[:, 0:T], wz[:, kc, m * 128:(m + 1) * 128], ub[:, kc, 0:T], start=(kc == 0), stop=(kc == 7))
                act(zs[:, m, 0:T], pb_[:, 0:T], AF.Silu)
            S.phase = 'inproj_x_conv'
            wx = nslab()
            wbc = nslab()
            deferred = []
            for j, ch in enumerate(chs):
                if kind == "p" and ti == 0:
                    mset("pool", xbc[:, j % 2, 0:3], 0.0)
                elif kind == "p":
                    cp("pool", xbc[:, j % 2, 0:3], convst[:, ch, 0:3])
                else:
                    cp("pool", xbc[:, j % 2, 0:3], convst[:, ch, 0:3])
                pb_ = nps()
                wsl = wx[:, :, j * 128:(j + 1) * 128] if j < 4 else wbc[:, :, (j - 4) * 128:(j - 3) * 128]
                for kc in range(8):
                    mm(pb_[:, 0:T], wsl[:, kc, :], ub[:, kc, 0:T], start=(kc == 0), stop=(kc == 7))
                cp("act", xbc[:, j % 2, 3:3 + T], pb_[:, 0:T])
                dst = xsT[:, j, :] if j < 4 else (BTf[:] if j == 4 else CTf[:])
                act(dst[:, 0:T], pb_[:, 0:T], AF.Identity, bias=pcol[:, PC_CB + ch:PC_CB + ch + 1], scale=cwcol(3, ch))
                if j % 2 == 1:
                    ts("pool", stmp[:, 0:T], xbc[:, j % 2, 0:T], cwcol(0, ch), None, ALU.mult)
                for k in range(1 if j % 2 == 1 else 0, 3):
                    stt("dve", dst[:, 0:T], xbc[:, j % 2, k:k + T], cwcol(k, ch), dst[:, 0:T], ALU.mult, ALU.add)
                if j % 2 == 1:
                    tt("pool", dst[:, 0:T], dst[:, 0:T], stmp[:, 0:T], ALU.add)
                def tail(j=j, ch=ch, dst=dst):
                    act(dst[:, 0:T], dst[:, 0:T], AF.Silu)
                    if j == 4:
                        cp("act", BTb[:, 0:T], BTf[:, 0:T])
                    if j == 5:
                        cp("act", CTb[:, 0:T], CTf[:, 0:T])
                    if kind == "p" and not last_p:
                        cp("pool", convst[:, ch, 0:3], xbc[:, j % 2, T:T + 3])
                    if need_out:
                        pc_ = nps()
                        tr(pc_[0:3, 0:128], xbc[:, j % 2, T:T + 3], 128)
                        cr_ = crow[j % 2]
                        cp("dve", cr_[0:3, :], pc_[0:3, 0:128])
                        dma("sp", (convp_d if kind == "p" else convs_d)[b, :, ch * 128:(ch + 1) * 128], cr_[0:3, :])
                if deferred:
                    deferred.pop(0)()
                deferred.append(tail)
            while deferred:
                deferred.pop(0)()
            S.phase = 'ssd'
            def pre(blk):
                t0 = blk * Q
                first = (kind == "p" and ti == 0 and blk == 0)
                sl = blk % 2
                Mb, Cs, xd, xdp, Btok, dec = Mb2[sl], Cs2[sl], xd2[sl], xdp2[sl], Btok2[sl], dec2[sl]
                dtg = dt_all[0:Q, blk, g * 8:(g + 1) * 8]
                tt("dve", dA[0:Q, :], dtg, A_bc[0:Q, g * 8:(g + 1) * 8], ALU.mult)
                for h8 in range(4):
                    act(rhsD[0:Q, h8, 0:Q], Umask[0:Q, 0:Q], AF.Identity, scale=dA[0:Q, h8:h8 + 1])
                tt("pool", rhsD[0:Q, 4:8, 0:Q], dA[0:Q, 4:8].unsqueeze(2).broadcast_to([Q, 4, Q]),
                   Umask[0:Q, 0:Q].unsqueeze(1).broadcast_to([Q, 4, Q]), ALU.mult)
                pb_ = nps()
                mm(pb_[0:Q, 0:8], Lb[0:Q, 0:Q], rhsD[0:Q, :, Q - 1], start=True, stop=True)
                mm(pb_[:, 8:16], onesB[0:Q, :], rhsD[0:Q, :, Q - 1], start=True, stop=True)
                act(dtd[0:Q, :], pb_[0:Q, 0:8], AF.Exp)
                act(dec[:], pb_[:, 8:16], AF.Exp)
                tt("dve", dtd[0:Q, :], dtd[0:Q, :], dtg, ALU.mult)
                pb_ = nps()
                mm(pb_[0:Q, 0:Q], BTb[:, t0:t0 + Q], CTb[:, t0:t0 + Q], start=True, stop=True)
                tt("dve", CBm[0:Q, 0:Q], pb_[0:Q, 0:Q], Umask[0:Q, 0:Q], ALU.mult)
                pb_ = nps()
                for m in range(4):
                    tr(pb_[0:Q, m * 128:(m + 1) * 128], xsT[:, m, t0:t0 + Q], 128)
                pv = pb_[0:Q, :].rearrange("p (h d) -> p h d", h=8)
                tt("dve", xd[0:Q, :].rearrange("p (h d) -> p h d", h=8), pv, dtg.unsqueeze(2).broadcast_to([Q, 8, HP]), ALU.mult)
                tt("dve", xdp[0:Q, :].rearrange("p (h d) -> p h d", h=8), pv, dtd[0:Q, :].unsqueeze(2).broadcast_to([Q, 8, HP]), ALU.mult)
                pb_ = nps()
                tr(pb_[0:Q, 0:128], BTf[:, t0:t0 + Q], 128)
                cp("act", Btok[0:Q, :], pb_[0:Q, 0:128])
                for half in range(2):
                    hs = slice(half * 4, half * 4 + 4)
                    pd = nps()
                    mm(pd[0:Q, 0:4 * Q], Lb[0:Q, 0:Q], rhsD[0:Q, hs, 0:Q], start=True, stop=True)
                    act(Ebuf[0:Q, :, 0:Q], pd[0:Q, 0:4 * Q].rearrange("p (h q) -> p h q", h=4), AF.Exp)
                    tt(("dve", "pool")[half], Mb[0:Q, hs, 0:Q], Ebuf[0:Q, :, 0:Q], CBm[0:Q, 0:Q].unsqueeze(1).broadcast_to([Q, 4, Q]), ALU.mult)
                    if not first:
                        pe_ = nps()
                        mm(pe_[:, 0:4 * Q], onesB[0:Q, :], rhsD[0:Q, hs, 0:Q], start=True, stop=True)
                        act(Evbuf[:, :, 0:Q], pe_[:, 0:4 * Q].rearrange("p (h q) -> p h q", h=4), AF.Exp)
                        tt(("pool", "dve")[half], Cs[:, hs, 0:Q], Evbuf[:, :, 0:Q], CTf[:, t0:t0 + Q].unsqueeze(1).broadcast_to([128, 4, Q]), ALU.mult)

            def post_state(blk):
                t0 = blk * Q
                first = (kind == "p" and ti == 0 and blk == 0)
                sl = blk % 2
                Mb, Cs, xd, xdp, Btok, dec = Mb2[sl], Cs2[sl], xd2[sl], xdp2[sl], Btok2[sl], dec2[sl]
                hv = hT[:, g * 8:(g + 1) * 8, :]
                hb_new = hTb2[sl][:, g * 8:(g + 1) * 8, :]
                hb_old = hTb2[1 - sl]
                sv = stmp[:].rearrange("p (h d) -> p h d", h=8)
                if not first:
                    tt("pool", sv, hv, dec[:].unsqueeze(2).broadcast_to([128, 8, HP]), ALU.mult)
                pst = nps()
                mm(pst[:, :], Btok[0:Q, :], xdp[0:Q, :], start=True, stop=True)
                psv = pst[:, :].rearrange("p (h d) -> p h d", h=8)
                if first:
                    cp("dve", hv, psv)
                else:
                    tt("dve", hv, sv, psv, ALU.add)
                cp("act", hb_new, hv)

            def post_y(blk):
                t0 = blk * Q
                first = (kind == "p" and ti == 0 and blk == 0)
                sl = blk % 2
                Mb, Cs, xd = Mb2[sl], Cs2[sl], xd2[sl]
                hb_old = hTb2[1 - sl]
                py = nps()
                for m in range(4):
                    for j in range(2):
                        hl = m * 2 + j
                        hg = g * 8 + hl
                        o_ = py[j * 64:(j + 1) * 64, m * Q:(m + 1) * Q]
                        mm(o_, xd[0:Q, hl * 64:(hl + 1) * 64], Mb[0:Q, hl, 0:Q], start=True, stop=first)
                        if not first:
                            mm(o_, hb_old[:, hg, :], Cs[:, hl, 0:Q], start=False, stop=True)
                for m in range(4):
                    stt("dve", xsT[:, m, t0:t0 + Q], xsT[:, m, t0:t0 + Q], Dcol[:, g * 4 + m:g * 4 + m + 1],
                        py[:, m * Q:(m + 1) * Q], ALU.mult, ALU.add)

            pre(0)
            for blk in range(nblk):
                post_state(blk)
                if blk + 1 < nblk:
                    pre(blk + 1)
                post_y(blk)
            for m in range(4):
                tt(("pool", "dve")[m % 2], xsT[:, m, 0:T], xsT[:, m, 0:T], zs[:, m, 0:T], ALU.mult)
            S.phase = 'gnorm'
            norm_r([xsT[:, m, 0:T] for m in range(4)], T, 512, rA)
            for m in range(4):
                i = PC_GN + g * 4 + m
                stt(ew2(), ynorm[:, g * 4 + m, 0:T], xsT[:, m, 0:T], pcol[:, i:i + 1], rA[:, 0:T], ALU.mult, ALU.mult)
        if need_out:
            for c4 in range(4):
                pb_ = nps()
                for i in range(4):
                    c = c4 * 4 + i
                    tr(pb_[:, i * 128:(i + 1) * 128], hT[:, 2 * c:2 * c + 2, :], 128)
                cp(("act", "dve")[c4 % 2], stst[:, c4 * 4:(c4 + 1) * 4, :], pb_[:, :].rearrange("p (c n) -> p c n", c=4))
            dma("sp", (ssmp_d if kind == "p" else ssms_d)[b].rearrange("(c p) n -> p c n", p=128), stst[:])

        def post_residual(lyr, j, acc_in, nxt_acc=None):
            acc_in.finish(D, rA)
            for c in range(8):
                tn = tmpn3[c % 3]
                stt("dve", tn[:, 0:T], mT[:, c, 0:T], ncol(lyr, j, c), rA[:, 0:T], ALU.mult, ALU.mult)
                tt(("pool", "dve")[c % 2], xT[:, c, 0:T], xT[:, c, 0:T], tn[:, 0:T], ALU.add)
                if nxt_acc is not None:
                    nxt_acc.add(xT[:, c, 0:T], eng="act")

        S.phase = 'outproj'
        acc_o = NormAcc(8, T)
        for mg in range(4):
            w_ = nslab()
            for mi in range(2):
                m = mg * 2 + mi
                pb_ = nps()
                for kc in range(16):
                    mm(pb_[:, 0:T], w_[:, kc, mi * 128:(mi + 1) * 128], ynorm[:, kc, 0:T], start=(kc == 0), stop=(kc == 15))
                cp(("act", "dve")[m % 2], mT[:, m, 0:T], pb_[:, 0:T])
                acc_o.add(mT[:, m, 0:T])
        if upto == 'outproj_nores':
            return len(per_tile) + wbase
        S.phase = 'post0'
        acc_f0 = NormAcc(8, T)
        post_residual(0, 1, acc_o, acc_f0)
        if upto == 'outproj':
            return len(per_tile) + wbase

        def ffn(lyr, acc_pre, nxt_acc):
            acc_pre.finish(D, rA)
            acc_m = NormAcc(8, T)
            for c in range(8):
                stt(ew2(), ub[:, c, 0:T], xT[:, c, 0:T], ncol(lyr, 2, c), rA[:, 0:T], ALU.mult, ALU.mult)
            for j in range(6):
                wg = nslab()
                wu = nslab()
                for mi in range(4 if j < 5 else 2):
                    m = j * 4 + mi
                    pg = nps()
                    for kc in range(8):
                        mm(pg[:, 0:T], wg[:, kc, mi * 128:(mi + 1) * 128], ub[:, kc, 0:T], start=(kc == 0), stop=(kc == 7))
                    pu = nps()
                    for kc in range(8):
                        mm(pu[:, 0:T], wu[:, kc, mi * 128:(mi + 1) * 128], ub[:, kc, 0:T], start=(kc == 0), stop=(kc == 7))
                    s_ = sg[m % 2]
                    act(s_[:, 0:T], pg[:, 0:T], AF.Silu)
                    tt("dve", hid[:, m, 0:T], s_[:, 0:T], pu[:, 0:T], ALU.mult)
            for mg in range(4):
                wA = nslab()
                wB = nslab()
                for mi in range(2):
                    m = mg * 2 + mi
                    pb_ = nps()
                    for kc in range(22):
                        w_ = wA[:, kc, mi * 128:(mi + 1) * 128] if kc < 11 else wB[:, kc - 11, mi * 128:(mi + 1) * 128]
                        mm(pb_[:, 0:T], w_, hid[:, kc, 0:T], start=(kc == 0), stop=(kc == 21))
                    cp(("act", "dve")[m % 2], mT[:, m, 0:T], pb_[:, 0:T])
                    acc_m.add(mT[:, m, 0:T])
            post_residual(lyr, 3, acc_m, nxt_acc)

        if upto == 'mixer':
            return len(per_tile) + wbase
        S.phase = 'ffn0'
        acc_kv = NormAcc(8, T)
        ffn(0, acc_f0, acc_kv)
        if upto == 'ffn0':
            return len(per_tile) + wbase
        dump("x1", xT[:, :, 0:T], (b * n_ptiles + ti) if kind == "p" else None)

        S.phase = 'kvq'
        cur = (ti % 2) if kind == "p" else 0
        prv = 1 - cur
        acc_kv.finish(D, rA)
        for c in range(8):
            stt(ew2(), u2[:, c, 0:T], xT[:, c, 0:T], pcol[:, PC_KVN + c:PC_KVN + c + 1], rA[:, 0:T], ALU.mult, ALU.mult)
            stt(ew2(), ub[:, c, 0:T], xT[:, c, 0:T], ncol(1, 0, c), rA[:, 0:T], ALU.mult, ALU.mult)
        if upto == 'l1norm':
            return len(per_tile) + wbase
        if kind == "s":
            dma("sp", KVst[:], ck_d[b].rearrange("(n p) d -> p n d", p=128))
            for c in range(8):
                pb_ = nps()
                for nb in range(4):
                    tr(pb_[:, nb * 128:(nb + 1) * 128], KVst[:, nb, c * 128:(c + 1) * 128], 128)
                cp(("act", "dve")[c % 2], KT[prv][:, c, :], pb_[:, :])
            dma("sp", KVst[:], cv_d[b].rearrange("(n p) d -> p n d", p=128))
            for nb in range(4):
                vsrc_ = KVst[:, nb, :].rearrange("p (a t d) -> p a t d", t=2, d=64)
                vdst_ = Vt[prv][:, nb, :].rearrange("p (a c) -> p a c", c=192)
                cp("act", vdst_[:, :, 0:64], vsrc_[:, :, 0, :])
                cp("pool", vdst_[:, :, 128:192], vsrc_[:, :, 1, :])
        kout = kp_d if kind == "p" else ks_d
        vout = vp_d if kind == "p" else vs_d
        for m2 in range(2):
            w_ = nslab()
            for mi in range(4):
                m = m2 * 4 + mi
                pb_ = nps()
                for kc in range(8):
                    mm(pb_[:, 0:T], w_[:, kc, mi * 128:(mi + 1) * 128], u2[:, kc, 0:T], start=(kc == 0), stop=(kc == 7))
                cp(("act", "dve")[m % 2], KT[cur][:, m, 0:T], pb_[:, 0:T])
            if need_out:
                for nb in range(NB):
                    pb_ = nps()
                    for kc in range(8):
                        mm(pb_[0:TBk, :], u2[:, kc, nb * 128:nb * 128 + TBk], w_[:, kc, :], start=(kc == 0), stop=(kc == 7))
                    cp(("act", "dve")[nb % 2], KVst[0:TBk, nb, m2 * 512:(m2 + 1) * 512], pb_[0:TBk, :])
        if upto == 'l1k_noout':
            return len(per_tile) + wbase
        if need_out:
            if kind == "p":
                dma("sp", kout[b].rearrange("(n p) d -> p n d", p=128), KVst[:])
            else:
                dma("sp", kout[b], KVst[0:64, 0, :])
        if upto == 'l1k':
            return len(per_tile) + wbase
        for m2 in range(2):
            w_ = nslab()
            for nb in range(NB):
                pb_ = nps()
                for kc in range(8):
                    mm(pb_[0:TBk, :], u2[:, kc, nb * 128:nb * 128 + TBk], w_[:, kc, :], start=(kc == 0), stop=(kc == 7))
                if need_out:
                    cp("dve", KVst[0:TBk, nb, m2 * 512:(m2 + 1) * 512], pb_[0:TBk, :])
                    vsrc_ = KVst[0:TBk, nb, m2 * 512:(m2 + 1) * 512].rearrange("p (a t d) -> p a t d", t=2, d=64)
                else:
                    vsrc_ = pb_[0:TBk, :].rearrange("p (a t d) -> p a t d", t=2, d=64)
                vdst_ = Vt[cur][0:TBk, nb, m2 * 768:(m2 + 1) * 768].rearrange("p (a c) -> p a c", c=192)
                e_ = "pool" if need_out else ("act", "dve")[nb % 2]
                cp(e_, vdst_[:, :, 0:64], vsrc_[:, :, 0, :])
                cp(e_, vdst_[:, :, 128:192], vsrc_[:, :, 1, :])
        if upto == 'l1v_noout':
            return len(per_tile) + wbase
        if need_out:
            if kind == "p":
                dma("sp", vout[b].rearrange("(n p) d -> p n d", p=128), KVst[:])
            else:
                dma("sp", vout[b], KVst[0:64, 0, :])
        if upto == 'l1v':
            return len(per_tile) + wbase
        for m2 in range(2):
            w_ = nslab()
            for mi in range(4):
                m = m2 * 4 + mi
                pb_ = nps()
                for kc in range(8):
                    mm(pb_[:, 0:T], w_[:, kc, mi * 128:(mi + 1) * 128], ub[:, kc, 0:T], start=(kc == 0), stop=(kc == 7))
                ts("dve", QT[:, m, 0:T], pb_[:, 0:T], 0.125, None, ALU.mult)
        if upto == 'l1proj':
            return len(per_tile) + wbase
        S.phase = 'attn'
        kbl = []
        if kind == "p":
            order = [4, 3, 2, 1, 0, 5, 6, 7] if ti > 0 else [4, 5, 6, 7]
            for jj in order:
                if jj >= 4:
                    q0 = 128 * (jj - 4)
                    kbl.append((cur, jj - 4, 128, q0, 512 - q0, 0))
                else:
                    kbl.append((prv, jj, 128, 0, 128 * (jj + 1), 128 * (4 - jj)))
        else:
            kbl.append((cur, 0, 64, 0, 64, 0))
            for jb in range(4):
                kbl.append((prv, jb, 128, 0, 64, 512 - 128 * jb))
        rot[0] = 4
        pi = 0
        for c in range(8):
            accs = [ps[4 + 2 * (c % 2)], ps[5 + 2 * (c % 2)]]
            items = [(ki, kbe, j) for ki, kbe in enumerate(kbl) for j in range(2)]
            pend = []

            def qk(ki, kbe, j):
                nonlocal pi
                (bf_, kb, nk, q0, nq, u0) = kbe
                h = 2 * c + j
                rows = slice(j * 64, (j + 1) * 64)
                pS = nps()
                mm(pS[0:nk, 0:nq], KT[bf_][rows, c, kb * 128:kb * 128 + nk], QT[rows, c, q0:q0 + nq], start=True, stop=False)
                mm(pS[0:nk, 0:nq], identB[:, 0:nk], TBb[:, h, u0:u0 + nq], start=False, stop=True)
                P_ = Pt[pi % 3]
                pi += 1
                act(P_[0:nk, 0:nq], pS[0:nk, 0:nq], AF.Exp)
                return P_

            def pv(ki, kbe, j, P_):
                (bf_, kb, nk, q0, nq, u0) = kbe
                h = 2 * c + j
                rows = slice(j * 64, (j + 1) * 64)
                lw = Vt[bf_][0:nk, kb, c * 192 + j * 64:c * 192 + j * 64 + 128]
                mm(accs[j][:, q0:q0 + nq], lw, P_[0:nk, 0:nq], start=(ki == 0), stop=(ki == len(kbl) - 1))

            for it in items:
                pend.append((it, qk(*it)))
                if len(pend) > 2:
                    it0, P0 = pend.pop(0)
                    pv(*it0, P0)
            for it0, P0 in pend:
                pv(*it0, P0)
            a0, a1 = accs
            S.add("dve", lambda e, a0=a0: e.reciprocal(out=rden[64:128, 0:T], in_=a0[64:128, 0:T]), reads=[a0[64:128, 0:T]], writes=[rden[64:128, 0:T]])
            S.add("dve", lambda e, a1=a1: e.reciprocal(out=rden[0:64, 0:T], in_=a1[0:64, 0:T]), reads=[a1[0:64, 0:T]], writes=[rden[0:64, 0:T]])
            tt("dve", OT[0:64, c, 0:T], a0[0:64, 0:T], rden[64:128, 0:T], ALU.mult)
            tt("dve", OT[64:128, c, 0:T], a1[64:128, 0:T], rden[0:64, 0:T], ALU.mult)
            if upto == 'attn1' and c == 0:
                return len(per_tile) + wbase
        rot[0] = 6
        if upto == 'attn':
            return len(per_tile) + wbase
        S.phase = 'oproj'
        acc_a = NormAcc(8, T)
        for m2 in range(2):
            w_ = nslab()
            for mi in range(4):
                m = m2 * 4 + mi
                pb_ = nps()
                for kc in range(8):
                    mm(pb_[:, 0:T], w_[:, kc, mi * 128:(mi + 1) * 128], OT[:, kc, 0:T], start=(kc == 0), stop=(kc == 7))
                cp(("act", "dve")[m % 2], mT[:, m, 0:T], pb_[:, 0:T])
                acc_a.add(mT[:, m, 0:T])
        if nxt is not None and upto == 'all':
            load_x(*nxt)
        acc_f1 = NormAcc(8, T)
        post_residual(1, 1, acc_a, acc_f1)
        S.phase = 'ffn1'
        ffn(1, acc_f1, None)
        S.phase = 'store'
        for nb in range(NB):
            for c2 in range(2):
                pb_ = nps()
                for i in range(4):
                    c = c2 * 4 + i
                    tr(pb_[0:TBk, i * 128:(i + 1) * 128], xT[:, c, nb * 128:nb * 128 + TBk], 128)
                cp(("act", "dve")[c2 % 2], xtok[0:TBk, nb, c2 * 512:(c2 + 1) * 512], pb_[0:TBk, :])
        if kind == "p":
            dma("sp", yp_d[b, ti * 512:(ti + 1) * 512, :].rearrange("(n p) d -> p n d", p=128), xtok[:])
        else:
            dma("sp", ys_d[b], xtok[0:64, 0, :])
        return wi[0]

    tiles = []
    for b in range(2):
        for ti in range(n_ptiles):
            tiles.append(("p", b, ti))
    if with_sample:
        tiles += [("s", 0, 0), ("s", 1, 0)]
    for _ in tiles:
        wq_list.extend(per_tile)
    wb = 0
    if upto == 'setup':
        tiles = []
        wq_list.clear()
        dma("sp", yp_d[0, 0:128, 0:640], TB32[:, 3, :])
    for i_, (kind, b, ti) in enumerate(tiles):
        wb = run_tile(kind, b, ti, wb, tiles[i_ + 1] if i_ + 1 < len(tiles) else None)
    assert wb == len(wq_list)
    print('max semvals', {e: sum(1 for o in S.ops[e] if o.needs_inc and not o.dma) for e in S.ENGS})

    S.emit(nc)
    est.close()
    _CACHE['S'] = S
    n_ops = {e: len(S.ops[e]) for e in S.ENGS}
    return nc, n_ops


IN_KEYS = ["xp", "xs", "sssm", "sconv", "ck", "cv", "normw", "win", "convw", "convb", "dtb", "alog", "dsk", "gnw",
           "wout", "kvnw", "wkv", "wq", "relb", "wo", "fin", "fout"]


def make_in_maps(inp, n_cores, LP):
    f = lambda a: np.ascontiguousarray(np.asarray(a, dtype=np.float32))
    shared = {
        "normw": f(inp["norm_w"]).reshape(64, 128),
        "win": f(inp["ssm_w_in"])[0],
        "convw": f(inp["ssm_conv_w"])[0].reshape(96, 128),
        "convb": f(inp["ssm_conv_b"])[0].reshape(24, 128),
        "dtb": f(inp["ssm_dt_bias"]).reshape(1, NH),
        "alog": f(inp["ssm_A_log"]).reshape(1, NH),
        "dsk": f(inp["ssm_D"]).reshape(1, NH),
        "gnw": f(inp["ssm_norm_w"])[0].reshape(16, 128),
        "wout": f(inp["ssm_w_out"])[0],
        "kvnw": f(inp["kv_norm_w"]).reshape(8, 128),
        "wkv": f(inp["w_kv"]),
        "wq": f(inp["attn_w_q"])[0],
        "relb": f(inp["attn_rel_bias"])[0],
        "wo": f(inp["attn_w_o"])[0],
        "fin": f(inp["ffn_w_in"]),
        "fout": f(inp["ffn_w_out"]),
    }
    maps = []
    for c in range(n_cores):
        sl = slice(2 * c, 2 * c + 2)
        m = dict(shared)
        m["xp"] = f(inp["x_prompt"][sl, :LP])
        m["xs"] = f(inp["x_sample"][sl])
        m["sssm"] = f(inp["state_ssm"][0, sl]).reshape(2, DI, NS)
        m["sconv"] = f(inp["state_conv"][0, sl])
        m["ck"] = f(inp["cache_k"][sl]).reshape(2, 512, D)
        m["cv"] = f(inp["cache_v"][sl]).reshape(2, 512, D)
        maps.append(m)
    return maps


def assemble(results, n_cores, LP):
    cat = lambda k: np.concatenate([np.asarray(r[k]) for r in results], axis=0)
    B = 2 * n_cores
    return (
        cat("yp").reshape(B, LP, D),
        cat("ys").reshape(B, 64, D),
        cat("ssmp").reshape(1, B, NH, HP, NS),
        cat("convp").reshape(1, B, 3, CD),
        cat("kp").reshape(B, 512, AH, 64),
        cat("vp").reshape(B, 512, AH, 64),
        cat("ssms").reshape(1, B, NH, HP, NS),
        cat("convs").reshape(1, B, 3, CD),
        cat("ks").reshape(B, 64, AH, 64),
        cat("vs").reshape(B, 64, AH, 64),
    )


def kernel(**inputs):
    if "nc" not in _CACHE:
        _CACHE["nc"] = build(4, True)[0]
    nc = _CACHE["nc"]
    maps = make_in_maps(inputs, 8, 2048)
    res = run_bass_kernel_spmd(nc, maps, core_ids=list(range(8)))
    return assemble(res.results, 8, 2048)
```

```python
import numpy as np
from contextlib import ExitStack
import concourse.bass as bass
import concourse.mybir as mybir
from concourse.ap import AP
from concourse.bass_utils import run_bass_kernel_spmd

F32 = mybir.dt.float32
BF16 = mybir.dt.bfloat16
ALU = mybir.AluOpType
AF = mybir.ActivationFunctionType

D = 1024
DI = 2048
NH = 32
HP = 64
NG = 4
NS = 128
CD = 3072
INP = 5152
FH = 2816
AH = 16
EPS = 1e-6
SB_BASE = 16512
SB_END = 229344
GRAN = 64
NEG = -30000.0


class Op:
    __slots__ = ("eng", "fn", "idx", "waits", "dwaits", "needs_inc", "semval", "dma", "dsem", "dval", "phase")

    def __init__(self, eng, fn, dma):
        self.eng = eng
        self.fn = fn
        self.dma = dma
        self.waits = []
        self.dwaits = []
        self.needs_inc = False
        self.semval = None
        self.dsem = None
        self.dval = None


class Sched:
    ENGS = ("pe", "act", "dve", "pool", "sp")
    NDSEM = 24

    def __init__(self):
        self.ops = {e: [] for e in self.ENGS}
        self.lastw = {}
        self.readers = {}
        self.waited = {e: {e2: -1 for e2 in self.ENGS} for e in self.ENGS}
        self.dwaited = {e: {} for e in self.ENGS}
        self.dcount = {e: 0 for e in self.ENGS}
        self.sb_addr = {}
        self.ps_names = set()
        self.phase = 'setup'

    def keys(self, item):
        if isinstance(item, (str, tuple)):
            return [item]
        name = item.name
        if name in self.sb_addr:
            base, ds = self.sb_addr[name]
            apl = item.ap
            pstep = apl[0][0]
            off = int(item.offset) % pstep if pstep > 0 else int(item.offset)
            lo = off
            hi = off
            for st, cnt in apl[1:]:
                if st >= 0:
                    hi += st * (cnt - 1)
                else:
                    lo += st * (cnt - 1)
            b0 = (base + lo * ds) // GRAN
            b1 = (base + (hi + 1) * ds - 1) // GRAN
            return [("s", g) for g in range(b0, b1 + 1)]
        if name in self.ps_names:
            return [("p", name)]
        return [("d", name, int(item.offset))]

    def add(self, eng, fn, reads=(), writes=(), dma=False):
        op = Op(eng, fn, dma)
        op.idx = len(self.ops[eng])
        op.phase = self.phase
        rk = []
        for r in reads:
            rk.extend(self.keys(r))
        wk = []
        for w in writes:
            wk.extend(self.keys(w))
        deps = []
        for k in rk:
            w = self.lastw.get(k)
            if w is not None:
                deps.append(w)
        for k in wk:
            w = self.lastw.get(k)
            if w is not None:
                deps.append(w)
            deps.extend(self.readers.get(k, ()))
        need = {}
        for d in deps:
            if d.dma:
                key = (d.eng, d.dsem)
                if self.dwaited[eng].get(key, 0) < d.dval:
                    self.dwaited[eng][key] = d.dval
                    op.dwaits.append((d.eng, d.dsem, d.dval))
            else:
                if d.eng == eng and eng == "pe":
                    continue
                if need.get(d.eng, -1) < d.idx:
                    need[d.eng] = d.idx
        for e2, idx in need.items():
            if self.waited[eng][e2] < idx:
                self.waited[eng][e2] = idx
                p = self.ops[e2][idx]
                p.needs_inc = True
                op.waits.append(p)
        if dma:
            n = self.dcount[eng]
            self.dcount[eng] = n + 1
            op.dsem = n % self.NDSEM
            op.dval = 16 * (n // self.NDSEM + 1)
            if n >= self.NDSEM:
                key = (eng, op.dsem)
                if self.dwaited[eng].get(key, 0) < op.dval - 16:
                    self.dwaited[eng][key] = op.dval - 16
                    op.dwaits.append((eng, op.dsem, op.dval - 16))
        for k in rk:
            self.readers.setdefault(k, []).append(op)
        for k in wk:
            self.lastw[k] = op
            self.readers[k] = []
        self.ops[eng].append(op)
        return op

    def emit(self, nc):
        with ExitStack() as st:
            esem = {e: st.enter_context(nc.semaphore("s_" + e)) for e in self.ENGS}
            dsem = {}
            for e in self.ENGS:
                if self.dcount[e] > 0:
                    dsem[e] = [st.enter_context(nc.semaphore("d_%s_%d" % (e, i)))
                               for i in range(min(self.NDSEM, self.dcount[e]))]
            for e in self.ENGS:
                c = 0
                for op in self.ops[e]:
                    if op.needs_inc and not op.dma:
                        c += 1
                        op.semval = c
            block = st.enter_context(nc.Block())

            def run(e):
                def body(h):
                    for op in self.ops[e]:
                        for p in op.waits:
                            h.wait_ge(esem[p.eng], p.semval)
                        for (qe, s, v) in op.dwaits:
                            h.wait_ge(dsem[qe][s], v)
                        ins = op.fn(h)
                        if op.dma:
                            ins.then_inc(dsem[e][op.dsem], 16)
                        elif op.needs_inc:
                            ins.then_inc(esem[e], 1)
                    if e in dsem:
                        n = self.dcount[e]
                        for s in range(len(dsem[e])):
                            cnt = (n - s + self.NDSEM - 1) // self.NDSEM
                            if cnt > 0:
                                h.wait_ge(dsem[e][s], 16 * cnt)
                return body
            block.tensor(run("pe"))
            block.scalar(run("act"))
            block.vector(run("dve"))
            block.gpsimd(run("pool"))
            block.sync(run("sp"))


_CACHE = {}


def build(n_ptiles=4, with_sample=True, dbg=None, upto='all', upto_s=None):
    LP = n_ptiles * 512
    nc = bass.Bass("TRN2", target_bir_lowering=False)
    S = Sched()

    def din(name, shape, dt=F32):
        return nc.dram_tensor(name, shape, dt, kind="ExternalInput").ap()

    def dout(name, shape):
        return nc.dram_tensor(name, shape, F32, kind="ExternalOutput").ap()

    def dscr(name, shape, dt):
        return nc.dram_tensor(name, shape, dt, kind="Internal").ap()

    xp_d = din("xp", [2, LP, D])
    xs_d = din("xs", [2, 64, D])
    sssm_d = din("sssm", [2, DI, NS])
    sconv_d = din("sconv", [2, 3, CD])
    ck_d = din("ck", [2, 512, D])
    cv_d = din("cv", [2, 512, D])
    normw_d = din("normw", [64, 128])
    win_d = din("win", [D, INP])
    convw_d = din("convw", [96, 128])
    convb_d = din("convb", [24, 128])
    dtb_d = din("dtb", [1, NH])
    alog_d = din("alog", [1, NH])
    dsk_d = din("dsk", [1, NH])
    gnw_d = din("gnw", [16, 128])
    wout_d = din("wout", [DI, D])
    kvnw_d = din("kvnw", [8, 128])
    wkv_d = din("wkv", [D, 2 * D])
    wq_d = din("wq", [D, D])
    relb_d = din("relb", [AH, 513])
    wo_d = din("wo", [D, D])
    fin_d = din("fin", [2, D, 2 * FH])
    fout_d = din("fout", [2, FH, D])

    yp_d = dout("yp", [2, LP, D])
    ys_d = dout("ys", [2, 64, D])
    ssmp_d = dout("ssmp", [2, DI, NS])
    convp_d = dout("convp", [2, 3, CD])
    kp_d = dout("kp", [2, 512, D])
    vp_d = dout("vp", [2, 512, D])
    ssms_d = dout("ssms", [2, DI, NS])
    convs_d = dout("convs", [2, 3, CD])
    ks_d = dout("ks", [2, 64, D])
    vs_d = dout("vs", [2, 64, D])
    dbg_d = {}
    if dbg:
        for k, (shp, dt_) in dbg.items():
            dbg_d[k] = nc.dram_tensor("dbg_" + k, shp, dt_, kind="ExternalOutput").ap()

    NSLAB = 90
    wscr = dscr("wscr", [NSLAB, 128, 4096], BF16)
    Fsc = dscr("Fsc", [AH, 128, 896], F32)

    pos = [SB_BASE]
    cnt = [0]

    def nbytes_of(shape, dt):
        ds = 2 if dt == BF16 else 4
        n = 1
        for s in shape:
            n *= s
        return (n * ds + 63) // 64 * 64

    def alloc(shape, dt, at=None):
        nb_ = nbytes_of(shape, dt)
        if at is None:
            addr = pos[0]
            pos[0] += nb_
        else:
            addr = at
        assert addr + nb_ <= SB_END, (addr, nb_, shape)
        cnt[0] += 1
        name = "t%d" % cnt[0]
        h = nc.alloc_sbuf_tensor_at(name, [128] + list(shape), dt, offset=addr)
        S.sb_addr[h[:].name] = (addr, 2 if dt == BF16 else 4)
        return h

    est = ExitStack()
    ps = [est.enter_context(nc.psum_tensor("psb%d" % i, [128, 512], F32)) for i in range(8)]
    for i in range(8):
        S.ps_names.add(ps[i][:].name)
    prot = [0]

    rot = [6]

    def nps():
        b_ = ps[prot[0] % rot[0]]
        prot[0] += 1
        return b_

    xT = alloc([8, 512], F32)
    hT = alloc([NH, HP], F32)
    hTb2 = [alloc([NH, HP], BF16) for _ in range(2)]
    convst = alloc([24, 4], F32)
    KT = [alloc([8, 512], BF16) for _ in range(2)]
    VW = 1536
    Vt = [alloc([4, VW], BF16) for _ in range(2)]
    TBb = alloc([AH, 640], BF16)
    identF = alloc([128], F32)
    identB = alloc([128], BF16)
    onesB = alloc([128], BF16)
    Umask = alloc([128], F32)
    Lb = alloc([128], BF16)
    pcol = alloc([256], F32)
    dtb_bc = alloc([NH], F32)
    A_bc = alloc([NH], F32)
    Dcol = alloc([16], F32)
    ub = alloc([8, 512], BF16)
    slabs = [alloc([4096], BF16) for _ in range(4)]
    rA = alloc([512], F32)
    lnt = alloc([512], F32)
    dt_all = alloc([4, NH], F32)
    arena0 = pos[0]

    class Ar:
        def __init__(self, start):
            self.p = start

        def al(self, shape, dt):
            t_ = alloc(shape, dt, at=self.p)
            self.p += nbytes_of(shape, dt)
            return t_

    A = Ar(arena0)
    sq = [A.al([512], BF16) for _ in range(2)]
    aftersq = A.p
    ynorm = A.al([16, 512], BF16)
    gstart = A.p
    zs = A.al([4, 512], F32)
    xbc = A.al([2, 528], F32)
    xsT = A.al([4, 512], F32)
    BTf = A.al([512], F32)
    CTf = A.al([512], F32)
    BTb = A.al([512], BF16)
    CTb = A.al([512], BF16)
    dA = A.al([8], F32)
    dtd = A.al([8], F32)
    rhsD = A.al([8, 128], BF16)
    Mb2 = [A.al([8, 128], BF16) for _ in range(2)]
    Cs2 = [A.al([8, 128], BF16) for _ in range(2)]
    CBm = A.al([128], F32)
    xd2 = [A.al([512], BF16) for _ in range(2)]
    xdp2 = [A.al([512], BF16) for _ in range(2)]
    Btok2 = [A.al([128], BF16) for _ in range(2)]
    dec2 = [A.al([8], F32) for _ in range(2)]
    Ebuf = A.al([4, 128], F32)
    Evbuf = A.al([4, 128], F32)
    stmp = A.al([512], F32)
    crow = [A.al([128], F32) for _ in range(2)]
    endA = A.p
    print('arena end A', endA, 'limit', SB_END)
    A2 = Ar(gstart)
    mT = A2.al([8, 512], F32)
    tmpn3 = [A2.al([512], F32) for _ in range(3)]
    afterm = A2.p
    hid = A2.al([22, 512], BF16)
    sg = [A2.al([512], F32) for _ in range(2)]
    endB = A2.p
    A3 = Ar(afterm)
    u2 = A3.al([8, 512], BF16)
    QT = A3.al([8, 512], BF16)
    Pt = [A3.al([512], BF16) for _ in range(3)]
    rden = A3.al([512], F32)
    KVst = alloc([4, D], F32, at=aftersq)
    endC = A3.p
    print('arena end B', endB, 'C', endC, 'A4', gstart + 16384 + 8192)
    OT = ynorm
    xin = alloc([4, D], F32, at=aftersq)
    A4 = Ar(gstart)
    xtok = A4.al([4, D], F32)
    stst = A4.al([16, 128], F32)
    A5 = Ar(aftersq)
    prow = A5.al([2, 128], F32)
    cbt = A5.al([AH, 1], F32)
    tailt = A5.al([AH, 383], F32)
    TB32 = A5.al([AH, 640], F32)

    PC_NORM, PC_KVN, PC_GN, PC_CB, PC_CW = 0, 64, 72, 88, 112

    def ncol(l, j, c):
        i = PC_NORM + (l * 4 + j) * 8 + c
        return pcol[:, i:i + 1]

    def cwcol(k, ch):
        i = PC_CW + k * 24 + ch
        return pcol[:, i:i + 1]

    def dma(eng, out, in_, reads=None, writes=None, **kw):
        return S.add(eng, lambda e: e.dma_start(out=out, in_=in_, **kw),
                     reads=[in_] if reads is None else reads,
                     writes=[out] if writes is None else writes, dma=True)

    def mm(out, lhsT, rhs, start=True, stop=True):
        return S.add("pe", lambda e: e.matmul(out, lhsT=lhsT, rhs=rhs, start=start, stop=stop),
                     reads=[lhsT, rhs], writes=[out])

    def tr(out, in_, n):
        return S.add("pe", lambda e: e.transpose(out, in_, identF[0:n, 0:n]), reads=[in_, identF[:]], writes=[out])

    def act(out, in_, func, bias=0.0, scale=1.0):
        rd = [in_]
        if not isinstance(bias, float):
            rd.append(bias)
        if not isinstance(scale, float):
            rd.append(scale)
        return S.add("act", lambda e: e.activation(out=out, in_=in_, func=func, bias=bias, scale=scale),
                     reads=rd, writes=[out])

    def tt(eng, out, in0, in1, op):
        return S.add(eng, lambda e: e.tensor_tensor(out=out, in0=in0, in1=in1, op=op), reads=[in0, in1], writes=[out])

    def stt(eng, out, in0, scalar, in1, op0, op1):
        eng = "dve"
        return S.add(eng, lambda e: e.scalar_tensor_tensor(out=out, in0=in0, scalar=scalar, in1=in1, op0=op0, op1=op1),
                     reads=[in0, scalar, in1], writes=[out])

    def ts(eng, out, in0, s1, s2, op0, op1=None):
        rd = [in0]
        if not isinstance(s1, float):
            rd.append(s1)
        if s2 is not None and not isinstance(s2, float):
            rd.append(s2)
        if op1 is None:
            return S.add(eng, lambda e: e.tensor_scalar(out=out, in0=in0, scalar1=s1, scalar2=None, op0=op0), reads=rd, writes=[out])
        return S.add(eng, lambda e: e.tensor_scalar(out=out, in0=in0, scalar1=s1, scalar2=s2, op0=op0, op1=op1), reads=rd, writes=[out])

    def cp(eng, out, in_):
        if eng == "act":
            return S.add("act", lambda e: e.copy(out=out, in_=in_), reads=[in_], writes=[out])
        return S.add(eng, lambda e: e.tensor_copy(out=out, in_=in_), reads=[in_], writes=[out])

    def mset(eng, out, val):
        return S.add(eng, lambda e: e.memset(out, val), writes=[out])

    rr = [0]

    def ew2():
        rr[0] += 1
        return ("dve", "pool")[rr[0] % 2]

    def dump(key, src, idx=None):
        if dbg and key in dbg_d:
            o = dbg_d[key] if idx is None else dbg_d[key][idx]
            dma("sp", o, src)

    cast_jobs = {}
    cast_done = set()

    def need_cast(keys):
        for k in keys:
            if k not in cast_done:
                cast_done.add(k)
                dst, src = cast_jobs[k]
                dma("pool", dst, src, writes=[k])

    def reg(key, dst, src):
        cast_jobs[key] = (dst, src)
        return [key]

    mset("pool", identF[:], 1.0)
    S.add("pool", lambda e: e.affine_select(out=identF[:], in_=identF[:], pattern=[[-1, 128]], compare_op=ALU.is_equal,
                                            fill=0.0, base=0, channel_multiplier=1), reads=[identF[:]], writes=[identF[:]])
    cp("dve", identB[:], identF[:])
    for vb in range(2):
        for nb_ in range(4):
            mset("pool", Vt[vb][:, nb_, :].rearrange("p (a c) -> p a c", c=192)[:, :, 64:128], 1.0)
    mset("dve", onesB[:], 1.0)
    mset("pool", Umask[:], 1.0)
    S.add("pool", lambda e: e.affine_select(out=Umask[:], in_=Umask[:], pattern=[[1, 128]], compare_op=ALU.is_ge,
                                            fill=0.0, base=0, channel_multiplier=-1), reads=[Umask[:]], writes=[Umask[:]])
    mset("pool", lnt[:, 0:128], 1.0)
    S.add("pool", lambda e: e.affine_select(out=lnt[:, 0:128], in_=lnt[:, 0:128], pattern=[[-1, 128]], compare_op=ALU.is_gt,
                                            fill=0.0, base=0, channel_multiplier=1), reads=[lnt[:, 0:128]], writes=[lnt[:, 0:128]])
    cp("dve", Lb[:], lnt[:, 0:128])

    mset("dve", prow[:], 0.0)
    dma("sp", prow[0:64, 0, :], normw_d)
    dma("sp", prow[64:72, 0, :], kvnw_d)
    dma("sp", prow[72:88, 0, :], gnw_d)
    dma("sp", prow[88:112, 0, :], convb_d)
    dma("sp", prow[112:128, 0, :], convw_d[0:16, :])
    dma("sp", prow[0:80, 1, :], convw_d[16:96, :])
    pb0 = nps()
    tr(pb0[:, 0:128], prow[:, 0, :], 128)
    tr(pb0[:, 128:256], prow[:, 1, :], 128)
    cp("dve", pcol[:], pb0[:, 0:256])

    dma("sp", dtb_bc[:], AP(dtb_d.tensor, 0, [[0, 128], [1, NH]]))
    dma("sp", A_bc[:], AP(alog_d.tensor, 0, [[0, 128], [1, NH]]))
    act(A_bc[:], A_bc[:], AF.Exp)
    ts("dve", A_bc[:], A_bc[:], -1.0, None, ALU.mult)
    dma("sp", Dcol[0:64, :], AP(dsk_d.tensor, 0, [[0, 64], [2, 16]]), allow_slow_non_contiguous=True)
    dma("sp", Dcol[64:128, :], AP(dsk_d.tensor, 1, [[0, 64], [2, 16]]), allow_slow_non_contiguous=True)

    dma("sp", Fsc[:, :, 0:513], relb_d.unsqueeze(1).broadcast_to([AH, 128, 513]), writes=["Fsc_a"])
    dma("sp", cbt[:], AP(relb_d.tensor, 512, [[0, 128], [513, AH], [1, 1]]), allow_slow_non_contiguous=True)
    cp("dve", tailt[:], cbt[:].broadcast_to([128, AH, 383]))
    dma("sp", Fsc[:, :, 513:896].rearrange("h p j -> p h j"), tailt[:], writes=["Fsc_b"])
    dma("sp", TB32[:], AP(Fsc.tensor, 256, [[895, 128], [128 * 896, AH], [1, 640]]), reads=["Fsc_a", "Fsc_b"])
    mset("dve", TB32[0:64, :, 576:640], NEG)
    mset("dve", TB32[64:128, :, 0:64], NEG)
    cp("dve", TBb[:], TB32[:])

    wq_list = []
    wstate = {"issued": 0}

    def slab_view(i, KC, w):
        return slabs[i % 4][:, 0:KC * w].rearrange("p (k w) -> p k w", k=KC)

    def cast_slab(si):
        if si in cast_done:
            return
        cast_done.add(si)
        KC, w, pieces = per_tile[si]
        dst = wscr[si][:, 0:KC * w].rearrange("p (k w) -> p k w", k=KC)
        for pi_, (src, c0, wp) in enumerate(pieces):
            dma("pool", dst[:, :, c0:c0 + wp], src.rearrange("(k p) w -> p k w", p=128), writes=[("wscr", si, pi_)])

    def issue_slab(i):
        npt = len(per_tile)
        for i2 in range(i, min(len(wq_list), i + 8)):
            cast_slab(i2 % npt)
        si = i % npt
        KC, w, pieces = per_tile[si]
        dma("sp", slabs[i % 4][:, 0:KC * w], wscr[si][:, 0:KC * w], reads=[("wscr", si, pi_) for pi_ in range(len(pieces))])

    def get_slab(i):
        while wstate["issued"] < min(len(wq_list), i + 3):
            issue_slab(wstate["issued"])
            wstate["issued"] += 1
        KC, w, _ = wq_list[i]
        return slab_view(i, KC, w)

    per_tile = []

    def tile_weight_list():
        L = []
        L.append((8, 32, [(win_d[:, INP - NH:INP], 0, NH)]))
        for g in range(NG):
            L.append((8, 512, [(win_d[:, g * 512:(g + 1) * 512], 0, 512)]))
            L.append((8, 512, [(win_d[:, DI + g * 512:DI + (g + 1) * 512], 0, 512)]))
            L.append((8, 256, [(win_d[:, 2 * DI + g * 128:2 * DI + (g + 1) * 128], 0, 128),
                               (win_d[:, 2 * DI + 512 + g * 128:2 * DI + 512 + (g + 1) * 128], 128, 128)]))
        for m in range(4):
            L.append((16, 256, [(wout_d[:, m * 256:(m + 1) * 256], 0, 256)]))
        for l in range(2):
            if l == 1:
                for m in range(4):
                    L.append((8, 512, [(wkv_d[:, m * 512:(m + 1) * 512], 0, 512)]))
                for m in range(2):
                    L.append((8, 512, [(wq_d[:, m * 512:(m + 1) * 512], 0, 512)]))
                for m in range(2):
                    L.append((8, 512, [(wo_d[:, m * 512:(m + 1) * 512], 0, 512)]))
            for j in range(6):
                w = 512 if j < 5 else 256
                L.append((8, w, [(fin_d[l][:, j * 512:j * 512 + w], 0, w)]))
                L.append((8, w, [(fin_d[l][:, FH + j * 512:FH + j * 512 + w], 0, w)]))
            for m in range(4):
                L.append((11, 256, [(fout_d[l][0:1408, m * 256:(m + 1) * 256], 0, 256)]))
                L.append((11, 256, [(fout_d[l][1408:2816, m * 256:(m + 1) * 256], 0, 256)]))
        assert len(L) <= NSLAB, len(L)
        return L

    per_tile.extend(tile_weight_list())

    nstat = [0, 0]

    class NormAcc:
        def __init__(self, n, T):
            self.bank = ps[6 + nstat[0] % 2]
            nstat[0] += 1
            self.i = 0
            self.n = n
            self.T = T
            self.pend = None

        def flush(self):
            if self.pend is not None:
                s_, k = self.pend
                mm(self.bank[:, 0:self.T], onesB[:], s_[:, 0:self.T], start=(k == 0), stop=(k == self.n - 1))
                self.pend = None

        def add(self, c, eng=None):
            T = self.T
            self.flush()
            s_ = sq[nstat[1] % 2]
            e = ("act", "pool", "dve")[nstat[1] % 3] if eng is None else eng
            nstat[1] += 1
            if e == "act":
                act(s_[:, 0:T], c, AF.Square)
            else:
                tt(e, s_[:, 0:T], c, c, ALU.mult)
            self.pend = (s_, self.i)
            self.i += 1

        def finish(self, Dn, rt):
            T = self.T
            assert self.i == self.n
            self.flush()
            act(lnt[:, 0:T], self.bank[:, 0:T], AF.Ln, bias=EPS, scale=1.0 / Dn)
            act(rt[:, 0:T], lnt[:, 0:T], AF.Exp, scale=-0.5)

    def norm_r(chunks, T, Dn, rt):
        a_ = NormAcc(len(chunks), T)
        for c in chunks:
            a_.add(c)
        a_.finish(Dn, rt)

    upto_g = upto
    xloaded = [False]

    def load_x(kind, b, ti):
        if kind == "p":
            dma("sp", xin[:], xp_d[b, ti * 512:(ti + 1) * 512, :].rearrange("(n p) d -> p n d", p=128))
        else:
            dma("sp", xin[0:64, 0, :], xs_d[b])
        xloaded[0] = True

    def run_tile(kind, b, ti, wbase, nxt=None):
        T = 512 if kind == "p" else 64
        TBk = min(T, 128)
        NB = (T + 127) // 128
        Q = TBk
        nblk = T // Q
        last_p = (kind == "p" and ti == n_ptiles - 1)
        need_out = last_p or kind == "s"
        wi = [wbase]
        upto = upto_g if (kind == "p" or upto_s is None) else upto_s

        def nslab():
            v = get_slab(wi[0])
            wi[0] += 1
            return v

        S.phase = 'load'
        if not xloaded[0]:
            load_x(kind, b, ti)
        xloaded[0] = False
        for c in range(8):
            pb_ = nps()
            for nb in range(NB):
                tr(pb_[:, nb * 128:nb * 128 + TBk], xin[0:TBk, nb, c * 128:(c + 1) * 128], TBk)
            cp(("act", "dve")[c % 2], xT[:, c, 0:T], pb_[:, 0:T])
        if kind == "s":
            cstage = xtok[0:3, :, :].rearrange("p a d -> p (a d)")
            dma("sp", cstage[:, 0:CD], sconv_d[b])
            pc_ = nps()
            for ch in range(24):
                tr(pc_[:, ch * 3:(ch + 1) * 3], cstage[:, ch * 128:(ch + 1) * 128], 3)
            cp("dve", convst[:, :, 0:3], pc_[:, 0:72].rearrange("p (c r) -> p c r", r=3))
            dma("sp", stst[:], sssm_d[b].rearrange("(c p) n -> p c n", p=128))
            for c4 in range(4):
                pb_ = nps()
                for i in range(4):
                    c = c4 * 4 + i
                    tr(pb_[:, i * 128:(i + 1) * 128], stst[:, c, :], 128)
                hv_ = hT[:, c4 * 8:(c4 + 1) * 8, :]
                cp("dve", hv_, pb_[:, :].rearrange("p (h d) -> p h d", h=8))
                cp("act", hTb2[1][:, c4 * 8:(c4 + 1) * 8, :], hv_)

        S.phase = 'norm0'
        norm_r([xT[:, c, 0:T] for c in range(8)], T, D, rA)
        for c in range(8):
            stt(ew2(), ub[:, c, 0:T], xT[:, c, 0:T], ncol(0, 0, c), rA[:, 0:T], ALU.mult, ALU.mult)
        if kind == "p" and b == 0 and ti == 0:
            dump("xT0", xT[:])
            dump("rA0", rA[:])
            dump("ub0", ub[:])
            dump("pcol", pcol[:])
        S.phase = 'dt'
        wdt = nslab()
        for nb in range(NB):
            pb_ = nps()
            for kc in range(8):
                mm(pb_[0:TBk, 0:NH], ub[:, kc, nb * 128:nb * 128 + TBk], wdt[:, kc, :], start=(kc == 0), stop=(kc == 7))
            tt("dve", dt_all[0:TBk, nb, :], pb_[0:TBk, 0:NH], dtb_bc[0:TBk, :], ALU.add)
        act(dt_all[0:TBk, 0:NB, :], dt_all[0:TBk, 0:NB, :], AF.Exp)
        act(dt_all[0:TBk, 0:NB, :], dt_all[0:TBk, 0:NB, :], AF.Ln, bias=1.0)

        for g in range(NG):
            chs = [g * 4 + m for m in range(4)] + [16 + g, 20 + g]
            S.phase = 'inproj_z'
            wz = nslab()
            for m in range(4):
                pb_ = nps()
                for kc in range(8):
                    mm(pb_[:, 0:T], wz[:, kc, m * 128:(m + 1) * 128], ub[:, kc, 0:T], start=(kc == 0), stop=(kc == 7))
                act(zs[:, m, 0:T], pb_[:, 0:T], AF.Silu)
            S.phase = 'inproj_x_conv'
            wx = nslab()
            wbc = nslab()
            deferred = []
            for j, ch in enumerate(chs):
                if kind == "p" and ti == 0:
                    mset("pool", xbc[:, j % 2, 0:3], 0.0)
                elif kind == "p":
                    cp("pool", xbc[:, j % 2, 0:3], convst[:, ch, 0:3])
                else:
                    cp("pool", xbc[:, j % 2, 0:3], convst[:, ch, 0:3])
                pb_ = nps()
                wsl = wx[:, :, j * 128:(j + 1) * 128] if j < 4 else wbc[:, :, (j - 4) * 128:(j - 3) * 128]
                for kc in range(8):
                    mm(pb_[:, 0:T], wsl[:, kc, :], ub[:, kc, 0:T], start=(kc == 0), stop=(kc == 7))
                cp("act", xbc[:, j % 2, 3:3 + T], pb_[:, 0:T])
                dst = xsT[:, j, :] if j < 4 else (BTf[:] if j == 4 else CTf[:])
                act(dst[:, 0:T], pb_[:, 0:T], AF.Identity, bias=pcol[:, PC_CB + ch:PC_CB + ch + 1], scale=cwcol(3, ch))
                for k in range(3):
                    stt("dve", dst[:, 0:T], xbc[:, j % 2, k:k + T], cwcol(k, ch), dst[:, 0:T], ALU.mult, ALU.add)
                def tail(j=j, ch=ch, dst=dst):
                    act(dst[:, 0:T], dst[:, 0:T], AF.Silu)
                    if j == 4:
                        cp("act", BTb[:, 0:T], BTf[:, 0:T])
                    if j == 5:
                        cp("act", CTb[:, 0:T], CTf[:, 0:T])
                    if kind == "p" and not last_p:
                        cp("pool", convst[:, ch, 0:3], xbc[:, j % 2, T:T + 3])
                    if need_out:
                        pc_ = nps()
                        tr(pc_[0:3, 0:128], xbc[:, j % 2, T:T + 3], 128)
                        cr_ = crow[j % 2]
                        cp("dve", cr_[0:3, :], pc_[0:3, 0:128])
                        dma("sp", (convp_d if kind == "p" else convs_d)[b, :, ch * 128:(ch + 1) * 128], cr_[0:3, :])
                if deferred:
                    deferred.pop(0)()
                deferred.append(tail)
            while deferred:
                deferred.pop(0)()
            S.phase = 'ssd'
            def pre(blk):
                t0 = blk * Q
                first = (kind == "p" and ti == 0 and blk == 0)
                sl = blk % 2
                Mb, Cs, xd, xdp, Btok, dec = Mb2[sl], Cs2[sl], xd2[sl], xdp2[sl], Btok2[sl], dec2[sl]
                dtg = dt_all[0:Q, blk, g * 8:(g + 1) * 8]
                tt("dve", dA[0:Q, :], dtg, A_bc[0:Q, g * 8:(g + 1) * 8], ALU.mult)
                for h8 in range(4):
                    act(rhsD[0:Q, h8, 0:Q], Umask[0:Q, 0:Q], AF.Identity, scale=dA[0:Q, h8:h8 + 1])
                tt("pool", rhsD[0:Q, 4:8, 0:Q], dA[0:Q, 4:8].unsqueeze(2).broadcast_to([Q, 4, Q]),
                   Umask[0:Q, 0:Q].unsqueeze(1).broadcast_to([Q, 4, Q]), ALU.mult)
                pb_ = nps()
                mm(pb_[0:Q, 0:8], Lb[0:Q, 0:Q], rhsD[0:Q, :, Q - 1], start=True, stop=True)
                mm(pb_[:, 8:16], onesB[0:Q, :], rhsD[0:Q, :, Q - 1], start=True, stop=True)
                act(dtd[0:Q, :], pb_[0:Q, 0:8], AF.Exp)
                act(dec[:], pb_[:, 8:16], AF.Exp)
                tt("dve", dtd[0:Q, :], dtd[0:Q, :], dtg, ALU.mult)
                pb_ = nps()
                mm(pb_[0:Q, 0:Q], BTb[:, t0:t0 + Q], CTb[:, t0:t0 + Q], start=True, stop=True)
                tt("dve", CBm[0:Q, 0:Q], pb_[0:Q, 0:Q], Umask[0:Q, 0:Q], ALU.mult)
                pb_ = nps()
                for m in range(4):
                    tr(pb_[0:Q, m * 128:(m + 1) * 128], xsT[:, m, t0:t0 + Q], 128)
                pv = pb_[0:Q, :].rearrange("p (h d) -> p h d", h=8)
                tt("dve", xd[0:Q, :].rearrange("p (h d) -> p h d", h=8), pv, dtg.unsqueeze(2).broadcast_to([Q, 8, HP]), ALU.mult)
                tt("dve", xdp[0:Q, :].rearrange("p (h d) -> p h d", h=8), pv, dtd[0:Q, :].unsqueeze(2).broadcast_to([Q, 8, HP]), ALU.mult)
                pb_ = nps()
                tr(pb_[0:Q, 0:128], BTf[:, t0:t0 + Q], 128)
                cp("act", Btok[0:Q, :], pb_[0:Q, 0:128])
                for half in range(2):
                    hs = slice(half * 4, half * 4 + 4)
                    pd = nps()
                    mm(pd[0:Q, 0:4 * Q], Lb[0:Q, 0:Q], rhsD[0:Q, hs, 0:Q], start=True, stop=True)
                    act(Ebuf[0:Q, :, 0:Q], pd[0:Q, 0:4 * Q].rearrange("p (h q) -> p h q", h=4), AF.Exp)
                    tt(("dve", "pool")[half], Mb[0:Q, hs, 0:Q], Ebuf[0:Q, :, 0:Q], CBm[0:Q, 0:Q].unsqueeze(1).broadcast_to([Q, 4, Q]), ALU.mult)
                    if not first:
                        pe_ = nps()
                        mm(pe_[:, 0:4 * Q], onesB[0:Q, :], rhsD[0:Q, hs, 0:Q], start=True, stop=True)
                        act(Evbuf[:, :, 0:Q], pe_[:, 0:4 * Q].rearrange("p (h q) -> p h q", h=4), AF.Exp)
                        tt(("pool", "dve")[half], Cs[:, hs, 0:Q], Evbuf[:, :, 0:Q], CTf[:, t0:t0 + Q].unsqueeze(1).broadcast_to([128, 4, Q]), ALU.mult)

            def post_state(blk):
                t0 = blk * Q
                first = (kind == "p" and ti == 0 and blk == 0)
                sl = blk % 2
                Mb, Cs, xd, xdp, Btok, dec = Mb2[sl], Cs2[sl], xd2[sl], xdp2[sl], Btok2[sl], dec2[sl]
                hv = hT[:, g * 8:(g + 1) * 8, :]
                hb_new = hTb2[sl][:, g * 8:(g + 1) * 8, :]
                hb_old = hTb2[1 - sl]
                sv = stmp[:].rearrange("p (h d) -> p h d", h=8)
                if not first:
                    tt("pool", sv, hv, dec[:].unsqueeze(2).broadcast_to([128, 8, HP]), ALU.mult)
                pst = nps()
                mm(pst[:, :], Btok[0:Q, :], xdp[0:Q, :], start=True, stop=True)
                psv = pst[:, :].rearrange("p (h d) -> p h d", h=8)
                if first:
                    cp("dve", hv, psv)
                else:
                    tt("dve", hv, sv, psv, ALU.add)
                cp("act", hb_new, hv)

            def post_y(blk):
                t0 = blk * Q
                first = (kind == "p" and ti == 0 and blk == 0)
                sl = blk % 2
                Mb, Cs, xd = Mb2[sl], Cs2[sl], xd2[sl]
                hb_old = hTb2[1 - sl]
                py = nps()
                for m in range(4):
                    for j in range(2):
                        hl = m * 2 + j
                        hg = g * 8 + hl
                        o_ = py[j * 64:(j + 1) * 64, m * Q:(m + 1) * Q]
                        mm(o_, xd[0:Q, hl * 64:(hl + 1) * 64], Mb[0:Q, hl, 0:Q], start=True, stop=first)
                        if not first:
                            mm(o_, hb_old[:, hg, :], Cs[:, hl, 0:Q], start=False, stop=True)
                for m in range(4):
                    stt("dve", xsT[:, m, t0:t0 + Q], xsT[:, m, t0:t0 + Q], Dcol[:, g * 4 + m:g * 4 + m + 1],
                        py[:, m * Q:(m + 1) * Q], ALU.mult, ALU.add)

            pre(0)
            for blk in range(nblk):
                post_state(blk)
                if blk + 1 < nblk:
                    pre(blk + 1)
                post_y(blk)
            for m in range(4):
                tt(("pool", "dve")[m % 2], xsT[:, m, 0:T], xsT[:, m, 0:T], zs[:, m, 0:T], ALU.mult)
            S.phase = 'gnorm'
            norm_r([xsT[:, m, 0:T] for m in range(4)], T, 512, rA)
            for m in range(4):
                i = PC_GN + g * 4 + m
                stt(ew2(), ynorm[:, g * 4 + m, 0:T], xsT[:, m, 0:T], pcol[:, i:i + 1], rA[:, 0:T], ALU.mult, ALU.mult)
        if need_out:
            for c4 in range(4):
                pb_ = nps()
                for i in range(4):
                    c = c4 * 4 + i
                    tr(pb_[:, i * 128:(i + 1) * 128], hT[:, 2 * c:2 * c + 2, :], 128)
                cp(("act", "dve")[c4 % 2], stst[:, c4 * 4:(c4 + 1) * 4, :], pb_[:, :].rearrange("p (c n) -> p c n", c=4))
            dma("sp", (ssmp_d if kind == "p" else ssms_d)[b].rearrange("(c p) n -> p c n", p=128), stst[:])

        def post_residual(lyr, j, acc_in, nxt_acc=None):
            acc_in.finish(D, rA)
            for c in range(8):
                tn = tmpn3[c % 3]
                stt("dve", tn[:, 0:T], mT[:, c, 0:T], ncol(lyr, j, c), rA[:, 0:T], ALU.mult, ALU.mult)
                tt(("pool", "dve")[c % 2], xT[:, c, 0:T], xT[:, c, 0:T], tn[:, 0:T], ALU.add)
                if nxt_acc is not None:
                    nxt_acc.add(xT[:, c, 0:T], eng="act")

        S.phase = 'outproj'
        acc_o = NormAcc(8, T)
        for mg in range(4):
            w_ = nslab()
            for mi in range(2):
                m = mg * 2 + mi
                pb_ = nps()
                for kc in range(16):
                    mm(pb_[:, 0:T], w_[:, kc, mi * 128:(mi + 1) * 128], ynorm[:, kc, 0:T], start=(kc == 0), stop=(kc == 15))
                cp(("act", "dve")[m % 2], mT[:, m, 0:T], pb_[:, 0:T])
                acc_o.add(mT[:, m, 0:T])
        if upto == 'outproj_nores':
            return len(per_tile) + wbase
        S.phase = 'post0'
        acc_f0 = NormAcc(8, T)
        post_residual(0, 1, acc_o, acc_f0)
        if upto == 'outproj':
            return len(per_tile) + wbase

        def ffn(lyr, acc_pre, nxt_acc):
            acc_pre.finish(D, rA)
            acc_m = NormAcc(8, T)
            for c in range(8):
                stt(ew2(), ub[:, c, 0:T], xT[:, c, 0:T], ncol(lyr, 2, c), rA[:, 0:T], ALU.mult, ALU.mult)
            for j in range(6):
                wg = nslab()
                wu = nslab()
                for mi in range(4 if j < 5 else 2):
                    m = j * 4 + mi
                    pg = nps()
                    for kc in range(8):
                        mm(pg[:, 0:T], wg[:, kc, mi * 128:(mi + 1) * 128], ub[:, kc, 0:T], start=(kc == 0), stop=(kc == 7))
                    pu = nps()
                    for kc in range(8):
                        mm(pu[:, 0:T], wu[:, kc, mi * 128:(mi + 1) * 128], ub[:, kc, 0:T], start=(kc == 0), stop=(kc == 7))
                    s_ = sg[m % 2]
                    act(s_[:, 0:T], pg[:, 0:T], AF.Silu)
                    tt("dve", hid[:, m, 0:T], s_[:, 0:T], pu[:, 0:T], ALU.mult)
            for mg in range(4):
                wA = nslab()
                wB = nslab()
                for mi in range(2):
                    m = mg * 2 + mi
                    pb_ = nps()
                    for kc in range(22):
                        w_ = wA[:, kc, mi * 128:(mi + 1) * 128] if kc < 11 else wB[:, kc - 11, mi * 128:(mi + 1) * 128]
                        mm(pb_[:, 0:T], w_, hid[:, kc, 0:T], start=(kc == 0), stop=(kc == 21))
                    cp(("act", "dve")[m % 2], mT[:, m, 0:T], pb_[:, 0:T])
                    acc_m.add(mT[:, m, 0:T])
            post_residual(lyr, 3, acc_m, nxt_acc)

        if upto == 'mixer':
            return len(per_tile) + wbase
        S.phase = 'ffn0'
        acc_kv = NormAcc(8, T)
        ffn(0, acc_f0, acc_kv)
        if upto == 'ffn0':
            return len(per_tile) + wbase
        dump("x1", xT[:, :, 0:T], (b * n_ptiles + ti) if kind == "p" else None)

        S.phase = 'kvq'
        cur = (ti % 2) if kind == "p" else 0
        prv = 1 - cur
        acc_kv.finish(D, rA)
        for c in range(8):
            stt(ew2(), u2[:, c, 0:T], xT[:, c, 0:T], pcol[:, PC_KVN + c:PC_KVN + c + 1], rA[:, 0:T], ALU.mult, ALU.mult)
            stt(ew2(), ub[:, c, 0:T], xT[:, c, 0:T], ncol(1, 0, c), rA[:, 0:T], ALU.mult, ALU.mult)
        if upto == 'l1norm':
            return len(per_tile) + wbase
        if kind == "s":
            dma("sp", KVst[:], ck_d[b].rearrange("(n p) d -> p n d", p=128))
            for c in range(8):
                pb_ = nps()
                for nb in range(4):
                    tr(pb_[:, nb * 128:(nb + 1) * 128], KVst[:, nb, c * 128:(c + 1) * 128], 128)
                cp(("act", "dve")[c % 2], KT[prv][:, c, :], pb_[:, :])
            dma("sp", KVst[:], cv_d[b].rearrange("(n p) d -> p n d", p=128))
            for nb in range(4):
                vsrc_ = KVst[:, nb, :].rearrange("p (a t d) -> p a t d", t=2, d=64)
                vdst_ = Vt[prv][:, nb, :].rearrange("p (a c) -> p a c", c=192)
                cp("act", vdst_[:, :, 0:64], vsrc_[:, :, 0, :])
                cp("pool", vdst_[:, :, 128:192], vsrc_[:, :, 1, :])
        kout = kp_d if kind == "p" else ks_d
        vout = vp_d if kind == "p" else vs_d
        for m2 in range(2):
            w_ = nslab()
            for mi in range(4):
                m = m2 * 4 + mi
                pb_ = nps()
                for kc in range(8):
                    mm(pb_[:, 0:T], w_[:, kc, mi * 128:(mi + 1) * 128], u2[:, kc, 0:T], start=(kc == 0), stop=(kc == 7))
                cp(("act", "dve")[m % 2], KT[cur][:, m, 0:T], pb_[:, 0:T])
            if need_out:
                for nb in range(NB):
                    pb_ = nps()
                    for kc in range(8):
                        mm(pb_[0:TBk, :], u2[:, kc, nb * 128:nb * 128 + TBk], w_[:, kc, :], start=(kc == 0), stop=(kc == 7))
                    cp(("act", "dve")[nb % 2], KVst[0:TBk, nb, m2 * 512:(m2 + 1) * 512], pb_[0:TBk, :])
        if upto == 'l1k_noout':
            return len(per_tile) + wbase
        if need_out:
            if kind == "p":
                dma("sp", kout[b].rearrange("(n p) d -> p n d", p=128), KVst[:])
            else:
                dma("sp", kout[b], KVst[0:64, 0, :])
        if upto == 'l1k':
            return len(per_tile) + wbase
        for m2 in range(2):
            w_ = nslab()
            for nb in range(NB):
                pb_ = nps()
                for kc in range(8):
                    mm(pb_[0:TBk, :], u2[:, kc, nb * 128:nb * 128 + TBk], w_[:, kc, :], start=(kc == 0), stop=(kc == 7))
                if need_out:
                    cp("dve", KVst[0:TBk, nb, m2 * 512:(m2 + 1) * 512], pb_[0:TBk, :])
                    vsrc_ = KVst[0:TBk, nb, m2 * 512:(m2 + 1) * 512].rearrange("p (a t d) -> p a t d", t=2, d=64)
                else:
                    vsrc_ = pb_[0:TBk, :].rearrange("p (a t d) -> p a t d", t=2, d=64)
                vdst_ = Vt[cur][0:TBk, nb, m2 * 768:(m2 + 1) * 768].rearrange("p (a c) -> p a c", c=192)
                e_ = "pool" if need_out else ("act", "dve")[nb % 2]
                cp(e_, vdst_[:, :, 0:64], vsrc_[:, :, 0, :])
                cp(e_, vdst_[:, :, 128:192], vsrc_[:, :, 1, :])
        if upto == 'l1v_noout':
            return len(per_tile) + wbase
        if need_out:
            if kind == "p":
                dma("sp", vout[b].rearrange("(n p) d -> p n d", p=128), KVst[:])
            else:
                dma("sp", vout[b], KVst[0:64, 0, :])
        if upto == 'l1v':
            return len(per_tile) + wbase
        for m2 in range(2):
            w_ = nslab()
            for mi in range(4):
                m = m2 * 4 + mi
                pb_ = nps()
                for kc in range(8):
                    mm(pb_[:, 0:T], w_[:, kc, mi * 128:(mi + 1) * 128], ub[:, kc, 0:T], start=(kc == 0), stop=(kc == 7))
                ts("dve", QT[:, m, 0:T], pb_[:, 0:T], 0.125, None, ALU.mult)
        if upto == 'l1proj':
            return len(per_tile) + wbase
        S.phase = 'attn'
        kbl = []
        if kind == "p":
            order = [4, 3, 2, 1, 0, 5, 6, 7] if ti > 0 else [4, 5, 6, 7]
            for jj in order:
                if jj >= 4:
                    q0 = 128 * (jj - 4)
                    kbl.append((cur, jj - 4, 128, q0, 512 - q0, 0))
                else:
                    kbl.append((prv, jj, 128, 0, 128 * (jj + 1), 128 * (4 - jj)))
        else:
            kbl.append((cur, 0, 64, 0, 64, 0))
            for jb in range(4):
                kbl.append((prv, jb, 128, 0, 64, 512 - 128 * jb))
        rot[0] = 4
        pi = 0
        for c in range(8):
            accs = [ps[4 + 2 * (c % 2)], ps[5 + 2 * (c % 2)]]
            items = [(ki, kbe, j) for ki, kbe in enumerate(kbl) for j in range(2)]
            pend = []

            def qk(ki, kbe, j):
                nonlocal pi
                (bf_, kb, nk, q0, nq, u0) = kbe
                h = 2 * c + j
                rows = slice(j * 64, (j + 1) * 64)
                pS = nps()
                mm(pS[0:nk, 0:nq], KT[bf_][rows, c, kb * 128:kb * 128 + nk], QT[rows, c, q0:q0 + nq], start=True, stop=False)
                mm(pS[0:nk, 0:nq], identB[:, 0:nk], TBb[:, h, u0:u0 + nq], start=False, stop=True)
                P_ = Pt[pi % 3]
                pi += 1
                act(P_[0:nk, 0:nq], pS[0:nk, 0:nq], AF.Exp)
                return P_

            def pv(ki, kbe, j, P_):
                (bf_, kb, nk, q0, nq, u0) = kbe
                h = 2 * c + j
                rows = slice(j * 64, (j + 1) * 64)
                lw = Vt[bf_][0:nk, kb, c * 192 + j * 64:c * 192 + j * 64 + 128]
                mm(accs[j][:, q0:q0 + nq], lw, P_[0:nk, 0:nq], start=(ki == 0), stop=(ki == len(kbl) - 1))

            for it in items:
                pend.append((it, qk(*it)))
                if len(pend) > 2:
                    it0, P0 = pend.pop(0)
                    pv(*it0, P0)
            for it0, P0 in pend:
                pv(*it0, P0)
            a0, a1 = accs
            S.add("dve", lambda e, a0=a0: e.reciprocal(out=rden[64:128, 0:T], in_=a0[64:128, 0:T]), reads=[a0[64:128, 0:T]], writes=[rden[64:128, 0:T]])
            S.add("dve", lambda e, a1=a1: e.reciprocal(out=rden[0:64, 0:T], in_=a1[0:64, 0:T]), reads=[a1[0:64, 0:T]], writes=[rden[0:64, 0:T]])
            tt("dve", OT[0:64, c, 0:T], a0[0:64, 0:T], rden[64:128, 0:T], ALU.mult)
            tt("dve", OT[64:128, c, 0:T], a1[64:128, 0:T], rden[0:64, 0:T], ALU.mult)
            if upto == 'attn1' and c == 0:
                return len(per_tile) + wbase
        rot[0] = 6
        if upto == 'attn':
            return len(per_tile) + wbase
        S.phase = 'oproj'
        acc_a = NormAcc(8, T)
        for m2 in range(2):
            w_ = nslab()
            for mi in range(4):
                m = m2 * 4 + mi
                pb_ = nps()
                for kc in range(8):
                    mm(pb_[:, 0:T], w_[:, kc, mi * 128:(mi + 1) * 128], OT[:, kc, 0:T], start=(kc == 0), stop=(kc == 7))
                cp(("act", "dve")[m % 2], mT[:, m, 0:T], pb_[:, 0:T])
                acc_a.add(mT[:, m, 0:T])
        if nxt is not None and upto == 'all':
            load_x(*nxt)
        acc_f1 = NormAcc(8, T)
        post_residual(1, 1, acc_a, acc_f1)
        S.phase = 'ffn1'
        ffn(1, acc_f1, None)
        S.phase = 'store'
        for nb in range(NB):
            for c2 in range(2):
                pb_ = nps()
                for i in range(4):
                    c = c2 * 4 + i
                    tr(pb_[0:TBk, i * 128:(i + 1) * 128], xT[:, c, nb * 128:nb * 128 + TBk], 128)
                cp(("act", "dve")[c2 % 2], xtok[0:TBk, nb, c2 * 512:(c2 + 1) * 512], pb_[0:TBk, :])
        if kind == "p":
            dma("sp", yp_d[b, ti * 512:(ti + 1) * 512, :].rearrange("(n p) d -> p n d", p=128), xtok[:])
        else:
            dma("sp", ys_d[b], xtok[0:64, 0, :])
        return wi[0]

    tiles = []
    for b in range(2):
        for ti in range(n_ptiles):
            tiles.append(("p", b, ti))
    if with_sample:
        tiles += [("s", 0, 0), ("s", 1, 0)]
    for _ in tiles:
        wq_list.extend(per_tile)
    wb = 0
    if upto == 'setup':
        tiles = []
        wq_list.clear()
        dma("sp", yp_d[0, 0:128, 0:640], TB32[:, 3, :])
    for i_, (kind, b, ti) in enumerate(tiles):
        wb = run_tile(kind, b, ti, wb, tiles[i_ + 1] if i_ + 1 < len(tiles) else None)
    assert wb == len(wq_list)
    print('max semvals', {e: sum(1 for o in S.ops[e] if o.needs_inc and not o.dma) for e in S.ENGS})

    S.emit(nc)
    est.close()
    _CACHE['S'] = S
    n_ops = {e: len(S.ops[e]) for e in S.ENGS}
    return nc, n_ops


IN_KEYS = ["xp", "xs", "sssm", "sconv", "ck", "cv", "normw", "win", "convw", "convb", "dtb", "alog", "dsk", "gnw",
           "wout", "kvnw", "wkv", "wq", "relb", "wo", "fin", "fout"]


def make_in_maps(inp, n_cores, LP):
    f = lambda a: np.ascontiguousarray(np.asarray(a, dtype=np.float32))
    shared = {
        "normw": f(inp["norm_w"]).reshape(64, 128),
        "win": f(inp["ssm_w_in"])[0],
        "convw": f(inp["ssm_conv_w"])[0].reshape(96, 128),
        "convb": f(inp["ssm_conv_b"])[0].reshape(24, 128),
        "dtb": f(inp["ssm_dt_bias"]).reshape(1, NH),
        "alog": f(inp["ssm_A_log"]).reshape(1, NH),
        "dsk": f(inp["ssm_D"]).reshape(1, NH),
        "gnw": f(inp["ssm_norm_w"])[0].reshape(16, 128),
        "wout": f(inp["ssm_w_out"])[0],
        "kvnw": f(inp["kv_norm_w"]).reshape(8, 128),
        "wkv": f(inp["w_kv"]),
        "wq": f(inp["attn_w_q"])[0],
        "relb": f(inp["attn_rel_bias"])[0],
        "wo": f(inp["attn_w_o"])[0],
        "fin": f(inp["ffn_w_in"]),
        "fout": f(inp["ffn_w_out"]),
    }
    maps = []
    for c in range(n_cores):
        sl = slice(2 * c, 2 * c + 2)
        m = dict(shared)
        m["xp"] = f(inp["x_prompt"][sl, :LP])
        m["xs"] = f(inp["x_sample"][sl])
        m["sssm"] = f(inp["state_ssm"][0, sl]).reshape(2, DI, NS)
        m["sconv"] = f(inp["state_conv"][0, sl])
        m["ck"] = f(inp["cache_k"][sl]).reshape(2, 512, D)
        m["cv"] = f(inp["cache_v"][sl]).reshape(2, 512, D)
        maps.append(m)
    return maps


def assemble(results, n_cores, LP):
    cat = lambda k: np.concatenate([np.asarray(r[k]) for r in results], axis=0)
    B = 2 * n_cores
    return (
        cat("yp").reshape(B, LP, D),
        cat("ys").reshape(B, 64, D),
        cat("ssmp").reshape(1, B, NH, HP, NS),
        cat("convp").reshape(1, B, 3, CD),
        cat("kp").reshape(B, 512, AH, 64),
        cat("vp").reshape(B, 512, AH, 64),
        cat("ssms").reshape(1, B, NH, HP, NS),
        cat("convs").reshape(1, B, 3, CD),
        cat("ks").reshape(B, 64, AH, 64),
        cat("vs").reshape(B, 64, AH, 64),
    )


def kernel(**inputs):
    if "nc" not in _CACHE:
        _CACHE["nc"] = build(4, True)[0]
    nc = _CACHE["nc"]
    maps = make_in_maps(inputs, 8, 2048)
    res = run_bass_kernel_spmd(nc, maps, core_ids=list(range(8)))
    return assemble(res.results, 8, 2048)
```
